# Optimizing a Trainium2 kernel written in Bass

```python
import jax, jax.numpy as jnp
from jax import lax
import numpy as np

D_MODEL = 1024
BATCH = 32
SEQ = 2048
DEPTH = 1

HGRN_HEADS = 8
HGRN_DK = 128
HGRN_DV = D_MODEL // HGRN_HEADS
HGRN_F = HGRN_HEADS * HGRN_DK
HGRN_V = HGRN_HEADS * HGRN_DV
CHUNK = 64
ATT_HEADS = 16
ATT_KV_HEADS = 2
ATT_HD = 64
ATT_GROUP = ATT_HEADS // ATT_KV_HEADS
WINDOW = 128
ROPE_DIM = ATT_HD // 4
ROPE_THETA = 500000.0
D_FF = 2816
CONV_W = 3
EPS = 1e-6
NEG_INF = -1e30
COL_SIZES = (HGRN_F, HGRN_F, HGRN_V, HGRN_V, ATT_HEADS * ATT_HD, ATT_KV_HEADS * ATT_HD, ATT_KV_HEADS * ATT_HD, D_MODEL, D_MODEL)
D_IN = HGRN_F + HGRN_F + HGRN_V + HGRN_V + ATT_HEADS * ATT_HD + 2 * ATT_KV_HEADS * ATT_HD + 2 * D_MODEL

kernel_name = 'hybrid_hgrn2_swa_convffn'


def rmsnorm(x, g):
    xf = x.astype(jnp.float32)
    y = xf * lax.rsqrt(jnp.mean(xf * xf, axis=-1, keepdims=True) + EPS)
    return (y * g.astype(jnp.float32)).astype(x.dtype)


def split_cols(z):
    idx = []
    acc = 0
    for s in COL_SIZES[:-1]:
        acc += s
        idx.append(acc)
    return jnp.split(z, idx, axis=-1)


def partial_rope(x, pos):
    half = ROPE_DIM // 2
    inv = ROPE_THETA ** (-2.0 * jnp.arange(half, dtype=jnp.float32) / ROPE_DIM)
    ang = pos.astype(jnp.float32)[..., None] * inv
    cos = jnp.cos(ang)[:, :, None, :]
    sin = jnp.sin(ang)[:, :, None, :]
    xr = x[..., :ROPE_DIM].astype(jnp.float32)
    x1, x2 = xr[..., :half], xr[..., half:]
    rot = jnp.concatenate([x1 * cos - x2 * sin, x2 * cos + x1 * sin], axis=-1).astype(x.dtype)
    return jnp.concatenate([rot, x[..., ROPE_DIM:]], axis=-1)


def hgrn2_chunkwise(q, fz, i, lb):
    B, S = q.shape[0], q.shape[1]
    n = S // CHUNK
    qf = jax.nn.silu(q.astype(jnp.float32))
    f = lb + (1.0 - lb) * jax.nn.sigmoid(fz.astype(jnp.float32))
    logf = jnp.log(f)
    k = 1.0 - f

    def chunks(t, d):
        return t.reshape(B, n, CHUNK, HGRN_HEADS, d).transpose(0, 3, 1, 2, 4)

    qc = chunks(qf, HGRN_DK)
    kc = chunks(k, HGRN_DK)
    vc = chunks(i.astype(jnp.float32), HGRN_DV)
    b = jnp.cumsum(chunks(logf, HGRN_DK), axis=3)
    bref = b[:, :, :, CHUNK // 2:CHUNK // 2 + 1]
    q_in = qc * jnp.exp(b - bref)
    k_in = kc * jnp.exp(bref - b)
    causal = jnp.tril(jnp.ones((CHUNK, CHUNK), dtype=bool))
    a = jnp.where(causal, jnp.einsum('bhncd,bhnsd->bhncs', q_in, k_in), 0.0)
    o_intra = jnp.einsum('bhncs,bhnse->bhnce', a, vc)
    blast = b[:, :, :, -1:]
    q_out = qc * jnp.exp(b)
    k_st = kc * jnp.exp(blast - b)
    dec = jnp.exp(blast[:, :, :, 0])

    def step(state, xs):
        qo, ks, vv, d = xs
        o = jnp.einsum('bhcd,bhde->bhce', qo, state)
        state = state * d[..., None] + jnp.einsum('bhcd,bhce->bhde', ks, vv)
        return state, o

    s0 = jnp.zeros((B, HGRN_HEADS, HGRN_DK, HGRN_DV), jnp.float32)
    mv = lambda t: jnp.moveaxis(t, 2, 0)
    _, o_inter = lax.scan(step, s0, (mv(q_out), mv(k_st), mv(vc), mv(dec)))
    o = o_intra + jnp.moveaxis(o_inter, 0, 2)
    return o.transpose(0, 2, 3, 1, 4).reshape(B, S, HGRN_HEADS, HGRN_DV)


def sliding_window_gqa(q, k, v, sinks, pos):
    B, S = q.shape[0], q.shape[1]
    nb = S // WINDOW
    q = partial_rope(q, pos)
    k = partial_rope(k, pos)
    qb = q.reshape(B, nb, WINDOW, ATT_KV_HEADS, ATT_GROUP, ATT_HD)

    def band_keys(t):
        tp = jnp.pad(t, ((0, 0), (WINDOW, 0), (0, 0), (0, 0))).reshape(B, nb + 1, WINDOW, ATT_KV_HEADS, ATT_HD)
        return jnp.concatenate([tp[:, :-1], tp[:, 1:]], axis=2)

    kb = band_keys(k)
    vb = band_keys(v)
    s = jnp.einsum('bnqkgd,bnmkd->bnkgqm', qb, kb).astype(jnp.float32) * (ATT_HD ** -0.5)
    qi = jnp.arange(WINDOW)[:, None]
    mi = jnp.arange(2 * WINDOW)[None, :]
    band = (mi > qi) & (mi <= qi + WINDOW)
    blk = jnp.arange(nb)[:, None, None]
    mask = band[None] & ((blk > 0) | (mi >= WINDOW)[None])
    s = jnp.where(mask[None, :, None, None], s, NEG_INF)
    sink = sinks.astype(jnp.float32).reshape(1, 1, ATT_KV_HEADS, ATT_GROUP, 1, 1)
    m = jnp.maximum(jnp.max(s, axis=-1, keepdims=True), sink)
    p = jnp.exp(s - m)
    den = jnp.sum(p, axis=-1, keepdims=True) + jnp.exp(sink - m)
    p = (p / den).astype(v.dtype)
    o = jnp.einsum('bnkgqm,bnmkd->bnqkgd', p, vb)
    return o.reshape(B, S, ATT_HEADS * ATT_HD)


def causal_dwconv(x, w, bias):
    y = lax.conv_general_dilated(x, w.astype(x.dtype)[:, None, :], window_strides=(1,), padding=[(CONV_W - 1, 0)], dimension_numbers=('NWC', 'WIO', 'NWC'), feature_group_count=x.shape[-1])
    return y + bias.astype(x.dtype)


def setup_inputs(seed: int = 0) -> dict:
    key = jax.random.key(seed)
    ks = jax.random.split(key, 18)
    nrm = lambda k, shape, fan_in: jax.random.normal(k, shape, jnp.float32) * (fan_in ** -0.5)
    x = jax.random.normal(ks[0], (BATCH, SEQ, D_MODEL), jnp.float32)
    offs = jax.random.randint(ks[1], (BATCH, 1), 0, 4096, dtype=jnp.int32)
    positions = (jnp.arange(SEQ, dtype=jnp.int32)[None, :] + offs).astype(jnp.int32)
    return {
        'x': x,
        'positions': positions,
        'norm1_g': 1.0 + 0.02 * jax.random.normal(ks[2], (DEPTH, D_MODEL), jnp.float32),
        'w_in': nrm(ks[3], (DEPTH, D_MODEL, D_IN), D_MODEL),
        'lb_logits': 0.1 * jax.random.normal(ks[4], (DEPTH + 1, HGRN_F), jnp.float32),
        'hgrn_norm_g': 1.0 + 0.02 * jax.random.normal(ks[5], (DEPTH, HGRN_DV), jnp.float32),
        'w_a': nrm(ks[6], (DEPTH, HGRN_V, D_MODEL), HGRN_V),
        'attn_sinks': 0.5 * jax.random.normal(ks[7], (DEPTH, ATT_HEADS), jnp.float32),
        'w_b': nrm(ks[8], (DEPTH, ATT_HEADS * ATT_HD, D_MODEL), ATT_HEADS * ATT_HD),
        'w_out': nrm(ks[9], (DEPTH, D_MODEL, D_MODEL), D_MODEL),
        'norm2_g': 1.0 + 0.02 * jax.random.normal(ks[10], (DEPTH, D_MODEL), jnp.float32),
        'w_ffn_in': nrm(ks[11], (DEPTH, D_MODEL, 2 * D_FF), D_MODEL),
        'conv_w': nrm(ks[12], (DEPTH, CONV_W, D_FF), CONV_W),
        'conv_b': 0.02 * jax.random.normal(ks[13], (DEPTH, D_FF), jnp.float32),
        'w_down': nrm(ks[14], (DEPTH, D_FF, D_MODEL), D_FF),
        'final_g': 1.0 + 0.02 * jax.random.normal(ks[15], (D_MODEL,), jnp.float32),
    }


def reference(x, positions, norm1_g, w_in, lb_logits, hgrn_norm_g, w_a, attn_sinks, w_b, w_out, norm2_g, w_ffn_in, conv_w, conv_b, w_down, final_g):
    B, S = x.shape[0], x.shape[1]
    lb_all = jnp.cumsum(jax.nn.softmax(lb_logits.astype(jnp.float32), axis=0), axis=0)
    h = x
    for l in range(DEPTH):
        u = rmsnorm(h, norm1_g[l])
        z = u @ w_in[l]
        hq, hf, hi, hg, aq, ak, av, ga, gb = split_cols(z)
        o_a = hgrn2_chunkwise(hq, hf, hi, lb_all[l])
        o_a = (rmsnorm(o_a, hgrn_norm_g[l]).reshape(B, S, HGRN_V) * jax.nn.silu(hg.astype(jnp.float32))).astype(x.dtype)
        o_b = sliding_window_gqa(aq.reshape(B, S, ATT_HEADS, ATT_HD), ak.reshape(B, S, ATT_KV_HEADS, ATT_HD), av.reshape(B, S, ATT_KV_HEADS, ATT_HD), attn_sinks[l], positions)
        merged = jax.nn.sigmoid(ga) * (o_a @ w_a[l]) + jax.nn.sigmoid(gb) * (o_b @ w_b[l])
        h = h + merged @ w_out[l]
        u = rmsnorm(h, norm2_g[l])
        gu = u @ w_ffn_in[l]
        g, up = gu[..., :D_FF], gu[..., D_FF:]
        a = causal_dwconv(g, conv_w[l], conv_b[l])
        h = h + (jax.nn.silu(a) * up) @ w_down[l]
    return rmsnorm(h, final_g)
```

```python
import numpy as np
import ml_dtypes
import concourse.bass as bass
import concourse.mybir as mybir
from concourse.bass_utils import run_bass_kernel_spmd

F32 = mybir.dt.float32
BF16 = mybir.dt.bfloat16
I32 = mybir.dt.int32
AF = mybir.ActivationFunctionType
ALU = mybir.AluOpType

NCORES = 8
SEQ = 2048
DM = 1024
TP = 512
NPASS_FULL = 16
D_IN = 7424
D_FF = 2816
NFF = 22
EPS = 1e-6
NEG = -30000.0


class Buf:
    __slots__ = ("w", "r", "name")

    def __init__(self, name=""):
        self.w = None
        self.r = {}
        self.name = name


class Prog:
    EPOCH = 30000

    def __init__(self, nc, n_dma_sems=24, self_sync=True):
        self.nc = nc
        self.E = {"pe": nc.tensor, "act": nc.scalar, "dve": nc.vector,
                  "pool": nc.gpsimd, "sp": nc.sync}
        self.sems = {e: [] for e in self.E}
        self.n = {e: 0 for e in self.E}
        self.seen = {e: {} for e in self.E}
        self.seen_d = {e: {} for e in self.E}
        self.n_hw = n_dma_sems
        self.n_sw = 8
        self.dsem = [nc.alloc_semaphore(f"dq{i}") for i in range(self.n_hw + self.n_sw)]
        self.dval = [0] * (self.n_hw + self.n_sw)
        self.dnext = {"hw": 0, "sw": 0}
        self.self_sync = self_sync

    def _sem(self, e, epoch):
        while len(self.sems[e]) <= epoch:
            self.sems[e].append(self.nc.alloc_semaphore(f"s_{e}_{len(self.sems[e])}"))
        return self.sems[e][epoch]

    def _wait(self, e, tok):
        if tok[0] == "c":
            _, oe, idx = tok
            if oe == e and not self.self_sync:
                return
            if self.seen[e].get(oe, -1) >= idx:
                return
            self.seen[e][oe] = idx
            self.E[e].wait_ge(self._sem(oe, idx // self.EPOCH), idx % self.EPOCH + 1)
        else:
            _, s, val = tok
            if self.seen_d[e].get(s, 0) >= val:
                return
            self.seen_d[e][s] = val
            self.E[e].wait_ge(self.dsem[s], val)

    def _deps(self, e, reads, writes, war_same=False):
        toks = []
        for b in reads:
            if b.w is not None:
                toks.append(b.w)
        for b in writes:
            if b.w is not None:
                toks.append(b.w)
            for t in b.r.values():
                toks.append(t)
        for t in toks:
            self._wait(e, t)

    def op(self, e, fn, reads=(), writes=(), touch=()):
        self._deps(e, reads, writes)
        ins = fn(self.E[e])
        idx = self.n[e]
        self.n[e] += 1
        ins.then_inc(self._sem(e, idx // self.EPOCH), 1)
        tok = ("c", e, idx)
        for b in reads:
            b.r[e] = tok
        for b in touch:
            b.r[e] = tok
        for b in writes:
            b.w = tok
            b.r = {}
        return tok

    def dma(self, q, out, in_, reads=(), writes=(), **kw):
        self._deps(q, reads, writes, war_same=True)
        if q == "pool":
            s = self.n_hw + self.dnext["sw"]
            self.dnext["sw"] = (self.dnext["sw"] + 1) % self.n_sw
        else:
            s = self.dnext["hw"]
            self.dnext["hw"] = (self.dnext["hw"] + 1) % self.n_hw
        if self.dval[s] > 0:
            self._wait(q, ("d", s, self.dval[s]))
        self.dval[s] += 16
        self.E[q].dma_start(out=out, in_=in_, **kw).then_inc(self.dsem[s], 16)
        tok = ("d", s, self.dval[s])
        for b in reads:
            b.r["dma%d" % s] = tok
        for b in writes:
            b.w = tok
            b.r = {}
        return tok

    def barrier(self):
        for e in self.E:
            for oe in self.E:
                if oe != e and self.n[oe] > 0:
                    self._wait(e, ("c", oe, self.n[oe] - 1))
            for s in range(len(self.dsem)):
                if self.dval[s]:
                    self._wait(e, ("d", s, self.dval[s]))


def _bf(a):
    return np.ascontiguousarray(a).astype(ml_dtypes.bfloat16)


def make_consts():
    c = {}
    c["c_ident"] = _bf(np.eye(128, dtype=np.float32))
    c["c_onesm"] = _bf(np.full((128, 128), 1.0 / 128, np.float32))
    s = np.arange(128)[:, None]
    q = np.arange(128)[None, :]
    hm = (s <= q).astype(np.float32)
    c["c_hmask4"] = _bf(np.tile(hm, (1, 4)))
    rm = np.ones((128, TP), np.float32)
    rm[:, ::128] = 0.0
    c["c_resetmask"] = _bf(rm)
    prev = np.where(s > q, 0.0, NEG).astype(np.float32)
    cur = np.where(s <= q, 0.0, NEG).astype(np.float32)
    full = np.full((128, 128), NEG, np.float32)
    c["c_maskN"] = _bf(np.concatenate([prev, cur, prev, cur], axis=1))
    c["c_maskF"] = _bf(np.concatenate([full, cur, full, cur], axis=1))
    half = 8
    inv = (np.float32(500000.0) ** (-2.0 * np.arange(half, dtype=np.float32) / np.float32(16))).astype(np.float32)
    invp = np.zeros((128, 1), np.float32)
    for p in range(128):
        d = p % 64
        if d < 16:
            invp[p, 0] = inv[d % 8]
    c["c_invp"] = invp
    Pm = np.zeros((128, 128), np.float32)
    for r0 in (0, 64):
        for j in range(8):
            Pm[r0 + j + 8, r0 + j] = -1.0
            Pm[r0 + j, r0 + j + 8] = 1.0
    c["c_pm"] = _bf(Pm)
    selA = np.zeros((4, 128, 128), np.float32)
    selB = np.zeros((4, 128, 128), np.float32)
    for kh in range(2):
        for hh in range(2):
            S = np.zeros((128, 128), np.float32)
            for i in range(64):
                S[kh * 64 + i, hh * 64 + i] = 1.0
            selA[kh * 2 + hh] = S
            selB[kh * 2 + hh] = Pm @ S
    c["c_selA"] = _bf(selA.transpose(1, 0, 2).reshape(128, 512))
    c["c_selB"] = _bf(selB.transpose(1, 0, 2).reshape(128, 512))
    ol = np.zeros((128, 256), np.float32)
    ol[:, 0:64] = 1.0
    ol[:, 128 + 64:256] = 1.0
    c["c_ones2"] = _bf(ol)
    return c


CONST_SHAPES = {
    "c_ident": ([128, 128], BF16), "c_onesm": ([128, 128], BF16), "c_hmask4": ([128, 512], BF16),
    "c_resetmask": ([128, TP], BF16), "c_maskN": ([128, 512], BF16), "c_maskF": ([128, 512], BF16),
    "c_invp": ([128, 1], F32), "c_pm": ([128, 128], BF16), "c_selA": ([128, 512], BF16),
    "c_selB": ([128, 512], BF16), "c_ones2": ([128, 256], BF16),
}

def make_units():
    units = []
    for g in range(2):
        units.append((f"UV{g}", [("w_in", 16 + 4 * g + i) for i in range(4)], list(range(8))))
    for h in range(8):
        units.append((f"UH{h}", [("w_in", h), ("w_in", 8 + h), ("w_in", 24 + h)], list(range(8))))
    units.append(("UQ0", [("w_in", 32 + i) for i in range(4)], list(range(8))))
    units.append(("UQ1", [("w_in", 36 + i) for i in range(4)], list(range(8))))
    units.append(("UKV", [("w_in", 40), ("w_in", 41)], list(range(8))))
    for j in range(8):
        units.append((f"UM{j}", [("w_in", 42 + j), ("w_in", 50 + j), ("w_a", j), ("w_b", j)], list(range(8))))
    for hf in range(2):
        units.append((f"UO{hf}", [("w_out", 4 * hf + i) for i in range(4)], list(range(8))))
    for c in range(NFF):
        units.append((f"UG{c}", [("w_ffn_in", c), ("w_ffn_in", 22 + c)], list(range(8))))
    for hf in range(2):
        for kp in range(2):
            units.append((f"UD{hf}{kp}", [("w_down", 4 * hf + i) for i in range(4)], list(range(11 * kp, 11 * kp + 11))))
    return units


WSHAPE = {"w_in": (1024, D_IN), "w_a": (1024, 1024), "w_b": (1024, 1024), "w_out": (1024, 1024),
          "w_ffn_in": (1024, 2 * D_FF), "w_down": (D_FF, 1024)}


def build_program(npass=NPASS_FULL, dbg=False):
    nc = bass.Bass("TRN2", target_bir_lowering=False)
    P = Prog(nc)
    dram_in = lambda name, shape, dt: nc.dram_tensor(name, shape, dt, kind="ExternalInput").ap()
    x_d = dram_in("x", [npass * TP, DM], F32)
    pos_d = dram_in("pos", [npass, TP], I32)
    W = {k: dram_in(k, list(v), F32) for k, v in WSHAPE.items()}
    gfb_d = dram_in("gfb", [128, DM], F32)
    lbl_d = dram_in("lbl", [128, 16], F32)
    gn_d = dram_in("gn", [128, 1], F32)
    sink_d = dram_in("sinkl", [128, 8], F32)
    cw_d = dram_in("cw", [128, NFF * 3], F32)
    cb_d = dram_in("cb", [128, NFF], F32)
    C_d = {k: dram_in(k, v[0], v[1]) for k, v in CONST_SHAPES.items()}
    y_d = nc.dram_tensor("y", [npass * TP, DM], F32, kind="ExternalOutput").ap()

    units = make_units()
    U_d = {}
    U_meta = {}
    segs = {}
    for name, chunks, kcs in units:
        U_d[name] = nc.dram_tensor("s_" + name, [128, len(chunks) * len(kcs) * 128], BF16, kind="Internal").ap()
        U_meta[name] = (chunks, len(kcs))
        for i, (w, jc) in enumerate(chunks):
            segs.setdefault((w, jc), []).append((name, i * len(kcs) * 128, kcs[0], len(kcs)))
    g1c_d = dram_in("g1c", [128, 8], F32)
    g2c_d = dram_in("g2c", [128, 8], F32)
    g1c = nc.alloc_sbuf_tensor("g1c_s", [128, 8], F32)
    g2c = nc.alloc_sbuf_tensor("g2c_s", [128, 8], F32)
    bgc = Buf()
    P.dma("sp", g1c[:], g1c_d, writes=[bgc])
    P.dma("sp", g2c[:], g2c_d, writes=[bgc])

    PW_MAX = 22528
    NLD = 4
    with nc.sbuf_tensor("stg_f", [128, NLD, 2048], F32) as stg_f, \
            nc.sbuf_tensor("stg_b", [128, 2, PW_MAX], BF16) as stg_b:
        bfl = [Buf() for _ in range(NLD)]
        bsb = [[Buf() for _ in range(22)] for _ in range(2)]
        ld = 0
        pci = 0
        for w, (K, N) in WSHAPE.items():
            KC = K // 128
            PW = 2048 if KC == 8 else 1024
            gcol = {"w_in": g1c, "w_ffn_in": g2c}.get(w)
            for c0 in range(0, N, PW):
                cw_ = min(PW, N - c0)
                nch = cw_ // 128
                si = pci % 2
                pci += 1
                blk = stg_b[:, si, 0:nch * KC * 128].rearrange("p (a b c) -> p a b c", b=KC, c=128)
                for kc in range(KC):
                    fi = ld % NLD
                    P.dma("sp", stg_f[:, fi, 0:cw_], W[w][kc * 128:(kc + 1) * 128, c0:c0 + cw_], writes=[bfl[fi]])
                    src = stg_f[:, fi, 0:cw_].rearrange("p (a c) -> p a c", c=128)
                    dst = blk[:, :, kc, :]
                    eng = "dve" if ld % 2 == 0 else "act"
                    if gcol is not None:
                        scale = gcol[:, kc:kc + 1]
                    elif w in ("w_a", "w_b"):
                        scale = 0.5
                    else:
                        scale = None
                    rd = [bfl[fi], bgc]
                    if eng == "dve":
                        if scale is None:
                            P.op("dve", lambda e, dst=dst, src=src: e.tensor_copy(dst, src), rd, [bsb[si][kc]])
                        else:
                            P.op("dve", lambda e, dst=dst, src=src, scale=scale: e.tensor_scalar(dst, src, scale, None, ALU.mult),
                                 rd, [bsb[si][kc]])
                    else:
                        if scale is None:
                            P.op("act", lambda e, dst=dst, src=src: e.activation(dst, src, AF.Copy), rd, [bsb[si][kc]])
                        elif isinstance(scale, float):
                            P.op("act", lambda e, dst=dst, src=src, scale=scale: e.activation(dst, src, AF.Copy, scale=scale),
                                 rd, [bsb[si][kc]])
                        else:
                            P.op("act", lambda e, dst=dst, src=src, scale=scale: e.activation(dst, src, AF.Identity, scale=scale),
                                 rd, [bsb[si][kc]])
                    ld += 1
                for jl in range(nch):
                    jc = c0 // 128 + jl
                    for (name, off, k0, nk) in segs.get((w, jc), []):
                        P.dma("pool", U_d[name][:, off:off + nk * 128],
                              blk[:, jl, k0:k0 + nk, :].rearrange("p a c -> p (a c)"),
                              reads=bsb[si][k0:k0 + nk], writes=[Buf()])
        P.barrier()

    sb = lambda name, shape, dt: nc.alloc_sbuf_tensor(name, shape, dt)
    cs = {k: sb(k + "_s", v[0], v[1]) for k, v in CONST_SHAPES.items()}
    bconst = Buf("const")
    gfb = sb("gfb_s", [128, DM], F32)
    lbl = sb("lbl_s", [128, 8, 2], F32)
    lbt = sb("lbt", [128, 8], F32)
    hs = sb("hs", [128, 8], F32)
    nhs = sb("nhs", [128, 8], F32)
    hb = sb("hb", [128, 8], F32)
    oml = sb("oml", [128, 8], F32)
    gn = sb("gn_s", [128, 1], F32)
    esink = sb("esink", [128, 8], F32)
    cw = sb("cw_s", [128, NFF, 3], F32)
    cb = sb("cb_s", [128, NFF], F32)

    h = sb("h", [128, 4, DM], F32)
    bh = [Buf(f"h{t}") for t in range(4)]
    uT = sb("uT", [128, 8, TP], BF16)
    buTt = [Buf(f"uT{t}") for t in range(4)]
    utok = [sb(f"utok{i}", [128, DM], BF16) for i in range(2)]
    butok = [Buf(), Buf()]
    ss = sb("ss", [128, 2], F32)
    bss = [Buf(), Buf()]
    ost = [sb(f"ost{i}", [128, DM], F32) for i in range(2)]
    bost = [Buf(), Buf()]
    posi_t = sb("posi", [128, TP], I32)
    kint_t = sb("kint", [128, TP], I32)
    bposi = Buf()
    bkint = Buf()
    ropeC = sb("ropeC", [128, TP], F32)
    ropeS = sb("ropeS", [128, TP], F32)
    brope = Buf()
    Rst = sb("Rst", [128, 8, 128], F32)
    Sbf = sb("Sbf", [128, 8, 128], BF16)
    bR = [Buf(f"R{i}") for i in range(8)]
    bS = [Buf("S0"), Buf("S1")]
    carry_sh = sb("carry_sh", [128, 8], F32)
    bcsh = Buf()
    Gall = sb("Gall", [128, 8, 8], F32)
    bG = Buf()
    sm = [sb(f"sm{i}", [128, 4, 8], F32) for i in range(2)]
    bsm = [Buf(), Buf()]
    X1 = sb("X1", [128, NFF * TP], BF16)
    X2 = sb("X2", [128, 3 * 8 * TP], BF16)
    x1v = lambda i: X1[:, i * 4 * TP:(i + 1) * 4 * TP].rearrange("p (a b) -> p a b", b=TP)
    q_in, k_in, q_out, shg, vg = x1v(0), x1v(1), x1v(2), x1v(3), x1v(4)
    bvg = Buf()
    bqk = [Buf() for _ in range(4)]
    ktok0 = [sb(f"ktok0_{i}", [128, 4, 128], BF16) for i in range(2)]
    bkt = [Buf(), Buf()]
    aTm = [X1[:, 20 * TP + i * 512:20 * TP + (i + 1) * 512].rearrange("p (a b) -> p a b", b=128) for i in range(2)]
    baT = [Buf(), Buf()]
    x2v = lambda i: X2[:, i * 8 * TP:(i + 1) * 8 * TP].rearrange("p (a b) -> p a b", b=TP)
    Qr, o_aT, o_bT = x2v(0), x2v(1), x2v(2)
    boa = Buf()
    bob = Buf()
    bQr = Buf()
    mT = Qr
    bmT = bQr
    Kpad = sb("Kpad", [128, 4, 640], BF16)
    bK = Buf()
    Vpad = sb("Vpad", [128, 4, 5, 128], BF16)
    bV = Buf()
    actT = X1[:, :].rearrange("p (a b) -> p a b", b=TP)
    bact_l = bqk + [bvg] + baT
    ccarry = sb("ccarry", [128, NFF, 2], F32)
    bcc = Buf()
    gbuf = [sb(f"gbuf{i}", [128, TP + 2], F32) for i in range(2)]
    bgb = [Buf(), Buf()]
    NT = 5
    tmpf = [sb(f"tmpf{i}", [128, TP], F32) for i in range(NT)]
    btf = [Buf() for _ in range(NT)]
    NH = 10
    htmp = [sb(f"htmp{i}", [128, TP], F32) for i in range(NH)]
    bht = [Buf() for _ in range(NH)]
    hfree = list(range(NH))

    def HA():
        i = hfree.pop(0)
        return htmp[i], bht[i], i

    def HF(i):
        assert i not in hfree
        hfree.append(i)
    NTB = 6
    tmpb = [sb(f"tmpb{i}", [128, TP], BF16) for i in range(NTB)]
    btb = [Buf() for _ in range(NTB)]
    tix = [0, 0]

    def TF():
        i = tix[0] % NT
        tix[0] += 1
        return tmpf[i], btf[i]

    def TB():
        i = tix[1] % NTB
        tix[1] += 1
        return tmpb[i], btb[i]

    WH = [sb(f"WH{i}", [128, 3, 8, 128], BF16) for i in range(2)]
    bWH = [Buf(), Buf()]
    WV = sb("WV", [128, 4, 8, 128], BF16)
    bWV = Buf()
    NWA = 3
    WA = [sb(f"WA{i}", [128, 4, 8, 128], BF16) for i in range(NWA)]
    bWA = [Buf() for _ in range(NWA)]
    WD = [X2[:, i * 5632:(i + 1) * 5632].rearrange("p (a b c) -> p a b c", b=11, c=128) for i in range(2)]
    bWD = [Buf("WD0"), Buf("WD1")]
    bWD_alias = [[bQr, boa], [boa, bob]]
    bactc = [Buf(f"act{c}") for c in range(NFF)]
    print("sbuf bytes remaining", nc.sbuf_bytes_remaining)

    NPB = 7
    pbk = [nc.alloc_psum_tensor(f"pb{i}", [128, 512], F32) for i in range(NPB)]
    bpb = [Buf() for _ in range(NPB)]
    ptall = nc.alloc_psum_tensor("ptall", [128, 1024], BF16)
    _bpt = Buf()
    ptk = [ptall[:, 0:512], ptall[:, 0:512]]
    bpt = [_bpt, _bpt]
    pix = [0, 0]

    pb_lim = [NPB]

    def PB():
        i = pix[0] % pb_lim[0]
        pix[0] += 1
        return pbk[i], bpb[i]

    def PT():
        i = pix[1] % 2
        pix[1] += 1
        return ptk[i], bpt[i]

    def act(out, in_, func, r, w, **kw):
        return P.op("act", lambda e: e.activation(out, in_, func, **kw), r, w)

    def tt(eng, out, a, b, op, r, w):
        return P.op(eng, lambda e: e.tensor_tensor(out, a, b, op), r, w)

    def ts(eng, out, a, s1, s2, op0, op1, r, w):
        if s2 is None:
            return P.op(eng, lambda e: e.tensor_scalar(out, a, s1, None, op0), r, w)
        return P.op(eng, lambda e: e.tensor_scalar(out, a, s1, s2, op0, op1), r, w)

    def stt(eng, out, a, s, b, op0, op1, r, w):
        return P.op(eng, lambda e: e.scalar_tensor_tensor(out, a, s, b, op0, op1), r, w)

    def cp(eng, out, in_, r, w):
        if eng == "act":
            return act(out, in_, AF.Copy, r, w)
        return P.op(eng, lambda e: e.tensor_copy(out, in_), r, w)

    def mms(items, r, w, touch=()):
        def fn(e):
            last = None
            for (o, l, rr, st, sp_) in items:
                last = e.matmul(o, l, rr, start=st, stop=sp_, skip_group_check=True)
            return last
        return P.op("pe", fn, r, w, touch)

    v3 = lambda ap, b: ap.rearrange("p (a b) -> p a b", b=b)
    dumped = {}

    def dump(name, ap, bufs):
        if not dbg or name in dumped:
            return
        d = nc.dram_tensor("d_" + name, list(ap.shape), ap.dtype, kind="ExternalOutput").ap()
        dumped[name] = d
        P.dma("sp", d, ap, reads=bufs, writes=[Buf()])

    def rstd_col(col, bcol, n):
        act(col, col, AF.Ln, [bcol], [bcol], bias=EPS, scale=1.0 / n)
        act(col, col, AF.Exp, [bcol], [bcol], scale=-0.5)

    for k in CONST_SHAPES:
        P.dma("sp", cs[k][:], C_d[k], writes=[bconst])
    bpar = Buf("par")
    P.dma("sp", gfb[:], gfb_d, writes=[bpar])
    P.dma("sp", lbl[:].rearrange("p a b -> p (a b)"), lbl_d, writes=[bpar])
    P.dma("sp", gn[:], gn_d, writes=[bpar])
    P.dma("sp", esink[:], sink_d, writes=[bpar])
    P.dma("sp", cw[:].rearrange("p a b -> p (a b)"), cw_d, writes=[bpar])
    P.dma("sp", cb[:], cb_d, writes=[bpar])
    tt("dve", lbt[:], lbl[:, :, 0], lbl[:, :, 1], ALU.subtract, [bpar], [bpar])
    act(lbt[:], lbt[:], AF.Tanh, [bpar], [bpar], scale=0.5)
    ts("dve", hs[:], lbt[:], -0.25, 0.25, ALU.mult, ALU.add, [bpar], [bpar])
    ts("dve", nhs[:], lbt[:], 0.25, -0.25, ALU.mult, ALU.add, [bpar], [bpar])
    ts("dve", hb[:], lbt[:], 0.25, 0.75, ALU.mult, ALU.add, [bpar], [bpar])
    act(esink[:], esink[:], AF.Exp, [bpar], [bpar])
    for i in range(2):
        P.op("pool", lambda e, i=i: e.memset(ktok0[i][:], 0.0), [], [bkt[i]])
    P.op("pool", lambda e: e.memset(Kpad[:], 0.0), [], [bK])
    P.op("pool", lambda e: e.memset(Vpad[:], 0.0), [], [bV])

    ident = cs["c_ident"]
    onesm = cs["c_onesm"]
    hmask4 = cs["c_hmask4"]
    resetmask = cs["c_resetmask"]
    maskN = cs["c_maskN"]
    maskF = cs["c_maskF"]
    invp = cs["c_invp"]
    pm = cs["c_pm"]
    selA = cs["c_selA"]
    selB = cs["c_selB"]
    ones2 = cs["c_ones2"]

    def pass_loads(pi):
        L = []
        L.append(("UQ0", "WA", None))
        L.append(("UQ1", "WA", None))
        L.append(("UKV", "WA", None))
        L.append(("UV0", "WV", 0))
        for hh in range(4):
            L.append((f"UH{hh}", "WH", hh % 2))
        L.append(("UV1", "WV", 0))
        for hh in range(4, 8):
            L.append((f"UH{hh}", "WH", hh % 2))
        for j in range(8):
            L.append((f"UM{j}", "WA", None))
        for hf in range(2):
            L.append((f"UO{hf}", "WA", None))
        for c in range(NFF):
            L.append((f"UG{c}", "WF", None))
        for hf in range(2):
            for kp in range(2):
                L.append((f"UD{hf}{kp}", "WD", kp))
        return L

    loads = []
    wa_ctr = 0
    for pi in range(npass):
        for (u, cls, idx) in pass_loads(pi):
            if cls == "WA":
                if wa_ctr % 2:
                    wa_ctr += 1
                idx = (wa_ctr // 2) % NWA
                wa_ctr += 2
            elif cls == "WF":
                idx = wa_ctr % (2 * NWA)
                wa_ctr += 1
            loads.append((u, cls, idx))

    def slots(ld):
        u, cls, idx = ld
        if cls == "WA":
            return {("WA", 2 * idx), ("WA", 2 * idx + 1)}
        if cls == "WF":
            return {("WA", idx)}
        return {(cls, idx)}
    lstate = {"issued": 0, "cur": 0}
    bWAh = [Buf(f"WAh{i}") for i in range(2 * NWA)]
    WFt = [WA[i // 2][:, (i % 2) * 2:(i % 2) * 2 + 2, :, :] for i in range(2 * NWA)]
    WBUF = {"WH": (WH, [[b] for b in bWH]), "WV": ([WV], [[bWV]]),
            "WA": (WA, [[bWAh[2 * i], bWAh[2 * i + 1]] for i in range(NWA)]),
            "WF": (WFt, [[b] for b in bWAh]), "WD": (WD, [[bWD[0]] + bWD_alias[0], [bWD[1]] + bWD_alias[1]])}

    def issue_load(k):
        u, cls, idx = loads[k]
        tiles, bufs = WBUF[cls]
        chunks, KC = U_meta[u]
        n = len(chunks) * KC * 128
        if cls == "WD":
            dst = X2[:, idx * 5632:idx * 5632 + n]
        elif cls == "WF":
            dst = WA[idx // 2][:].rearrange("p a b c -> p (a b c)")[:, (idx % 2) * 2048:(idx % 2) * 2048 + n]
        else:
            dst = tiles[idx][:].rearrange("p a b c -> p (a b c)")[:, 0:n]
        P.dma("sp", dst, U_d[u], writes=bufs[idx])

    def get_w(name, ahead=5, keep=0):
        k = lstate["cur"]
        assert loads[k][0] == name, (loads[k], name)
        lim = min(len(loads), k + 1 + ahead)
        while lstate["issued"] < lim:
            kk_ = lstate["issued"]
            if any(slots(loads[m]) & slots(loads[kk_]) for m in range(k - keep, kk_)):
                break
            issue_load(kk_)
            lstate["issued"] += 1
        assert lstate["issued"] > k
        lstate["cur"] += 1
        u, cls, idx = loads[k]
        tiles, bufs = WBUF[cls]
        return tiles[idx], bufs[idx]

    def norm_tile(t):
        k = t % 2
        act(ost[k][:], h[:, t, :], AF.Square, [bh[t]], [bost[k], bss[k]], accum_out=ss[:, k:k + 1])
        rstd_col(ss[:, k:k + 1], bss[k], DM)
        ts("dve", utok[k][:], h[:, t, :], ss[:, k:k + 1], None, ALU.mult, None,
           [bh[t], bss[k]], [butok[k]])
        for half in range(2):
            pt, bp = PT()

            def tr(e, k=k, half=half, pt=pt):
                last = None
                for i in range(4):
                    kc = half * 4 + i
                    last = e.transpose(pt[:, i * 128:(i + 1) * 128], utok[k][:, kc * 128:(kc + 1) * 128], ident[:])
                return last
            P.op("pe", tr, [butok[k], bconst], [bp])
            cp("act" if half == 0 else "dve", uT[:, half * 4:half * 4 + 4, t * 128:(t + 1) * 128],
               v3(pt[:, 0:512], 128), [bp], [buTt[t]])

    def norm_to_uT():
        for t in range(4):
            norm_tile(t)

    TWO_PI = float(2 * np.pi)
    PI_LO = 3.1415925

    def s_load_norm1(pi):
        first = (pi % 4 == 0)
        for t in range(4):
            r0 = pi * TP + t * 128
            P.dma("sp", h[:, t, :], x_d[r0:r0 + 128, :], writes=[bh[t]])
        posi = posi_t[:]
        kint = kint_t[:]
        P.dma("sp", posi, pos_d[pi:pi + 1, :].partition_broadcast(128), writes=[bposi])
        if first:
            P.op("pool", lambda e: e.memset(Rst[:], 0.0), [], bR)
            P.op("pool", lambda e: e.memset(Sbf[:], 0.0), [], bS)
            P.op("pool", lambda e: e.memset(carry_sh[:], 0.0), [], [bcsh])
            P.op("pool", lambda e: e.memset(ccarry[:], 0.0), [], [bcc])
        norm_to_uT()
        dump("uT", uT[:], buTt)
        posf, bposf = TF()
        cp("dve", posf[:], posi, [bposi], [bposf])
        for which, dst in ((0, ropeS), (1, ropeC)):
            ang, bang = TF()
            if which == 0:
                ts("dve", ang[:], posf[:], invp[:, 0:1], None, ALU.mult, None, [bposf, bconst], [bang])
            else:
                ts("dve", ang[:], posf[:], invp[:, 0:1], float(np.pi / 2), ALU.mult, ALU.add, [bposf, bconst], [bang])
            ts("dve", kint, ang[:], 1.0 / TWO_PI, None, ALU.mult, None, [bang], [bkint])
            kf, bkf = TF()
            cp("dve", kf[:], kint, [bkint], [bkf])
            stt("dve", ang[:], kf[:], -TWO_PI, ang[:], ALU.mult, ALU.add, [bkf, bang], [bang])
            ts("dve", ang[:], ang[:], PI_LO, -PI_LO, ALU.min, ALU.max, [bang], [bang])
            act(dst[:], ang[:], AF.Sin, [bang], [brope])

    def g_head(g, hl):
        hh = 4 * g + hl
        smi = sm[hl % 2]
        bsm_ = bsm[hl % 2]
        dd = smi[:, 0, 0:4]
        shv = smi[:, 1, 0:4]
        gg = smi[:, 2, 0:4]
        wh, bwh = get_w(f"UH{hh}")
        zs = []
        for ci in range(3):
            pb, bp = PB()
            mms([(pb[:], wh[:, ci, kc, :], uT[:, kc, :], kc == 0, kc == 7) for kc in range(8)],
                buTt + bwh, [bp])
            zs.append((pb, bp))
        (zq, bzq), (zf, bzf), (zg, bzg) = zs
        qf, bqf, iqf = HA()
        th, bth, ith = HA()
        lf, blf, ilf = HA()
        act(qf[:], zq[:], AF.Silu, [bzq], [bqf])
        act(shg[:, hl, :], zg[:], AF.Silu, [bzg], [bqk[hl]])
        act(th[:], zf[:], AF.Tanh, [bzf], [bth], scale=0.5)
        yield
        act(lf[:], th[:], AF.Ln, [bth, bpar], [blf], bias=hb[:, hh:hh + 1], scale=hs[:, hh:hh + 1])
        yield
        kk, bkk, ikk = HA()
        bb, bbb, ibb = HA()
        ts("dve", kk[:], th[:], nhs[:, hh:hh + 1], hs[:, hh:hh + 1], ALU.mult, ALU.add, [bth, bpar], [bkk])
        P.op("dve", lambda e: e.tensor_tensor_scan(bb[:], resetmask[:], lf[:], 0.0, ALU.mult, ALU.add),
             [blf, bconst], [bbb])
        b3 = v3(bb[:], 128)
        d1, bd1 = lf, blf
        tt("dve", v3(d1[:], 128), b3, b3[:, :, 64:65].broadcast_to([128, 4, 128]), ALU.subtract, [bbb], [bd1])
        tt("dve", dd, b3[:, :, 127], b3[:, :, 64], ALU.subtract, [bbb], [bsm_])
        cp("dve", shv[:, 0:1], carry_sh[:, hh:hh + 1], [bcsh, bsm_], [bsm_])
        cp("dve", shv[:, 1:4], dd[:, 0:3], [bsm_], [bsm_])
        cp("dve", carry_sh[:, hh:hh + 1], dd[:, 3:4], [bsm_], [bcsh])
        tt("dve", gg, b3[:, :, 64], shv, ALU.add, [bbb, bsm_], [bsm_])
        yield
        e2, be2 = th, bth
        act(e2[:], d1[:], AF.Exp, [bd1], [be2], scale=-1.0)
        act(Gall[:, hh, 0:4], gg, AF.Exp, [bsm_], [bG])
        tt("dve", k_in[:, hl, :], kk[:], e2[:], ALU.mult, [bkk, be2], [bqk[hl]])
        HF(ikk)
        HF(ith)
        e1, be1, ie1 = HA()
        act(e1[:], d1[:], AF.Exp, [bd1], [be1])
        tt("dve", q_in[:, hl, :], qf[:], e1[:], ALU.mult, [bqf, be1], [bqk[hl]])
        HF(ie1)
        d3, bd3 = d1, bd1
        tt("dve", v3(d3[:], 128), b3, shv.unsqueeze(2).broadcast_to([128, 4, 128]), ALU.add, [bbb, bsm_], [bd3])
        HF(ibb)
        yield
        e3, be3 = d3, bd3
        act(e3[:], d3[:], AF.Exp, [bd3], [be3])
        tt("dve", q_out[:, hl, :], qf[:], e3[:], ALU.mult, [bqf, be3], [bqk[hl]])
        HF(iqf)
        HF(ilf)
        yield

    def s_hgrn_proj(pi, g):
        wv, bwv = get_w(f"UV{g}")
        for t in range(4):
            pb, bp = PB()
            mms([(v3(pb[:], 128), uT[:, kc, t * 128:(t + 1) * 128], wv[:, :, kc, :], kc == 0, kc == 7)
                 for kc in range(8)], [buTt[t]] + bwv, [bp])
            cp("act", vg[:, t, :], pb[:], [bp], [bvg])
        gens = [g_head(g, hl) for hl in range(4)]
        sched = [(0, 0), (0, 1), (0, 2), (1, 0), (1, 1), (0, 3), (1, 2), (0, 4), (2, 0), (2, 1), (1, 3),
                 (2, 2), (1, 4), (3, 0), (3, 1), (2, 3), (3, 2), (2, 4), (3, 3), (3, 4)]
        for (hl, _seg) in sched:
            try:
                next(gens[hl])
            except StopIteration:
                pass
        dump("vg", vg, [bvg])
        dump("q_in", q_in, bqk)
        dump("k_in", k_in, bqk)
        dump("q_out", q_out, bqk)

    def g_rec(g, t):
        h0 = 4 * g
        tsl = slice(t * 128, (t + 1) * 128)
        par = t % 2
        pt, bp = PT()

        def trk(e, pt=pt, tsl=tsl):
            last = None
            for hl in range(4):
                last = e.transpose(pt[:, hl * 128:(hl + 1) * 128], k_in[:, hl, tsl], ident[:])
            return last
        P.op("pe", trk, bqk + [bconst], [bp])
        pA, bpA = pbk[0], bpb[0]
        mms([(pA[:, hl * 128:(hl + 1) * 128], k_in[:, hl, tsl], q_in[:, hl, tsl], True, True) for hl in range(4)],
            bqk, [bpA])
        yield
        cp("act", ktok0[par][:], v3(pt[:, 0:512], 128), [bp], [bkt[par]])
        tt("dve", aTm[par][:].rearrange("p a b -> p (a b)"), pA[:], hmask4[:], ALU.mult, [bpA, bconst], [baT[par]])
        pO, bpO = pbk[1], bpb[1]
        pK, bpK = pbk[2], bpb[2]
        items = []
        for hl in range(4):
            cs_ = slice(hl * 128, (hl + 1) * 128)
            items.append((pO[:, cs_], vg[:, t, cs_], aTm[par][:, hl, :], hl == 0, False))
            items.append((pO[:, cs_], Sbf[:, h0 + hl, :], q_out[:, hl, tsl], False, hl == 3))
        mms(items, [bvg, baT[par], bS[g]] + bqk, [bpO])
        mms([(pK[:, hl * 128:(hl + 1) * 128], ktok0[par][:, hl, :], vg[:, t, hl * 128:(hl + 1) * 128], True, True)
             for hl in range(4)], [bkt[par], bvg], [bpK])
        yield
        sq, bsq = TB()
        act(sq[:], pO[:], AF.Square, [bpO], [bsq])
        for hl in range(4):
            stt("dve", Rst[:, h0 + hl, :], Rst[:, h0 + hl, :], Gall[:, h0 + hl, t:t + 1],
                pK[:, hl * 128:(hl + 1) * 128], ALU.mult, ALU.add, [bR[h0 + hl], bG, bpK], [bR[h0 + hl]])
        cp("act", Sbf[:, h0:h0 + 4, :], Rst[:, h0:h0 + 4, :], bR[h0:h0 + 4], [bS[g]])
        pS, bpS = pbk[0], bpb[0]
        mms([(pS[:], onesm[:], sq[:], True, True)], [bsq, bconst], [bpS])
        yield
        rt, brt = TF()
        act(rt[:], pS[:], AF.Ln, [bpS], [brt], bias=EPS)
        act(rt[:], rt[:], AF.Exp, [brt], [brt], scale=-0.5)
        t1, bt1 = TF()
        stt("dve", t1[:], pO[:], gn[:, 0:1], rt[:], ALU.mult, ALU.mult, [bpO, brt, bpar], [bt1])
        tt("dve", o_aT[:, h0:h0 + 4, tsl], v3(t1[:], 128), shg[:, :, tsl], ALU.mult, [bt1] + bqk, [boa])

    def rope_split(z, bz):
        A, bA = TB()
        B, bB = TB()
        tt("dve", A[:], z[:], ropeC[:], ALU.mult, [bz, brope], [bA])
        tt("dve", B[:], z[:], ropeS[:], ALU.mult, [bz, brope], [bB])
        return A, bA, B, bB

    def s_attn_proj():
        pend = []

        def flush():
            kind, j, A, bA, B, bB = pend.pop(0)
            if kind == "q":
                pr, bpr = PB()
                mms([(pr[:], ident[:], A[:], True, False), (pr[:], pm[:], B[:], False, True)], [bA, bB, bconst], [bpr])
                act(Qr[:, j, :], pr[:], AF.Copy, [bpr], [bQr], scale=0.125)
            else:
                for var in range(4):
                    pr, bpr = PB()
                    mms([(pr[:], selA[:, var * 128:(var + 1) * 128], A[:], True, False),
                         (pr[:], selB[:, var * 128:(var + 1) * 128], B[:], False, True)], [bA, bB, bconst], [bpr])
                    cp("act", Kpad[:, var, 128:640], pr[:], [bpr], [bK])

        wkv = None
        for qi in range(2):
            wq, bwq = get_w(f"UQ{qi}")
            for jl in range(4):
                j = qi * 4 + jl
                pb, bp = PB()
                mms([(pb[:], wq[:, jl, kc, :], uT[:, kc, :], kc == 0, kc == 7) for kc in range(8)], buTt + bwq, [bp])
                A, bA, B, bB = rope_split(pb, bp)
                pend.append(("q", j, A, bA, B, bB))
                if len(pend) > 1:
                    flush()
        wkv, bwkv = get_w("UKV")
        pb, bp = PB()
        mms([(pb[:], wkv[:, 0, kc, :], uT[:, kc, :], kc == 0, kc == 7) for kc in range(8)], buTt + bwkv, [bp])
        A, bA, B, bB = rope_split(pb, bp)
        pend.append(("k", 0, A, bA, B, bB))
        flush()
        pb, bp = PB()
        items = []
        for t in range(4):
            for kc in range(8):
                items.append((pb[:, t * 128:(t + 1) * 128], uT[:, kc, t * 128:(t + 1) * 128], wkv[:, 1, kc, :],
                              (t == 0 and kc == 0), (t == 3 and kc == 7)))
        mms(items, buTt + bwkv, [bp])
        pb3 = v3(pb[:], 128)
        for kh in range(2):
            for hh in range(2):
                var = kh * 2 + hh
                cp("act" if hh == 0 else "dve", Vpad[:, var, 1:5, hh * 64:(hh + 1) * 64], pb3[:, :, kh * 64:(kh + 1) * 64], [bp], [bV])
        flush()
        dump("Qr", Qr, [bQr])
        dump("Kpad", Kpad[:], [bK])
        dump("Vpad", Vpad[:], [bV])

    def g_attn(pi, t, jb):
        first = (pi % 4 == 0)
        tsl = slice(t * 128, (t + 1) * 128)
        msk = maskF if (first and t == 0) else maskN
        pV, bpV = pbk[3], bpb[3]
        pD, bpD = pbk[4], bpb[4]
        kh = jb

        def pvden(jl, pT_, bpT):
            items = []
            items2 = []
            n_ = 0
            for hh in range(2):
                for kb in range(2):
                    r_ = (hh * 2 + kb)
                    st = (jl == 0 and n_ == 0)
                    items.append((pV[:, jl * 128:(jl + 1) * 128], Vpad[:, kh * 2 + hh, t + kb, :],
                                  pT_[:, r_ * 128:(r_ + 1) * 128], st, (jl == 3 and n_ == 3)))
                    items2.append((pD[:, jl * 128:(jl + 1) * 128], ones2[:, hh * 128:(hh + 1) * 128],
                                   pT_[:, r_ * 128:(r_ + 1) * 128], st, (jl == 3 and n_ == 3)))
                    n_ += 1
            mms(items, [bV, bpT], [bpV])
            mms(items2, [bconst, bpT], [bpD])

        prev = None
        for jl in range(4):
            j = jb * 4 + jl
            pSc, bpSc = pbk[5 + jl % 2], bpb[5 + jl % 2]
            items = [(pSc[:], ident[:], msk[:], True, False)]
            for hh in range(2):
                for kb in range(2):
                    r_ = (hh * 2 + kb)
                    items.append((pSc[:, r_ * 128:(r_ + 1) * 128], Kpad[:, kh * 2 + hh, (t + kb) * 128:(t + kb + 1) * 128],
                                  Qr[:, j, tsl], False, r_ == 3))
            mms(items, [bconst, bK, bQr], [bpSc])
            pT_, bpT = TB()
            act(pT_[:], pSc[:], AF.Exp, [bpSc], [bpT])
            if prev is not None:
                pvden(*prev)
            prev = (jl, pT_, bpT)
            yield
        pvden(*prev)
        yield
        dn, bdn = TF()
        for jl in range(4):
            j = jb * 4 + jl
            act(dn[:, jl * 128:(jl + 1) * 128], pD[:, jl * 128:(jl + 1) * 128], AF.Ln, [bpD, bpar], [bdn], bias=esink[:, j:j + 1])
        act(dn[:], dn[:], AF.Exp, [bdn], [bdn], scale=-1.0)
        tt("dve", o_bT[:, jb * 4:jb * 4 + 4, tsl], v3(pV[:], 128), v3(dn[:], 128), ALU.mult, [bpV, bdn], [bob])

    def interleave(gens):
        active = list(gens)
        while active:
            for g_ in list(active):
                try:
                    next(g_)
                except StopIteration:
                    active.remove(g_)

    def s_attn_carry():
        dump("o_bT", o_bT, [bob])
        cp("pool", Kpad[:, :, 0:128], Kpad[:, :, 512:640], [bK], [bK])
        cp("pool", Vpad[:, :, 0, :], Vpad[:, :, 4, :], [bV], [bV])

    def s_merge():
        dump("o_aT", o_aT, [boa])
        for j in range(8):
            wm, bwm = get_w(f"UM{j}")
            srcs = [(uT, buTt), (uT, buTt), (o_aT, [boa]), (o_bT, [bob])]
            ps_ = []
            for ci in range(4):
                pb, bp = PB()
                src, bsrc = srcs[ci]
                mms([(pb[:], wm[:, ci, kc, :], src[:, kc, :], kc == 0, kc == 7) for kc in range(8)], bsrc + bwm, [bp])
                ps_.append((pb, bp))
            ta, bta = TF()
            act(ta[:], ps_[0][0][:], AF.Tanh, [ps_[0][1]], [bta], scale=0.5)
            tb_, btb_ = TF()
            act(tb_[:], ps_[1][0][:], AF.Tanh, [ps_[1][1]], [btb_], scale=0.5)
            stt("dve", ta[:], ta[:], 1.0, ps_[2][0][:], ALU.add, ALU.mult, [bta, ps_[2][1]], [bta])
            stt("dve", tb_[:], tb_[:], 1.0, ps_[3][0][:], ALU.add, ALU.mult, [btb_, ps_[3][1]], [btb_])
            tt("dve", mT[:, j, :], ta[:], tb_[:], ALU.add, [bta, btb_], [bmT])
        dump("mT", mT[:], [bmT])

    def s_outproj_norm2():
        wos = [get_w("UO0"), get_w("UO1", keep=1)]
        for t in range(4):
            for hf in range(2):
                wo, bwo = wos[hf]
                pb, bp = PB()
                mms([(v3(pb[:], 128), mT[:, kc, t * 128:(t + 1) * 128], wo[:, :, kc, :], kc == 0, kc == 7) for kc in range(8)],
                    [bmT] + bwo, [bp])
                tt("dve", h[:, t, hf * 512:(hf + 1) * 512], pb[:], h[:, t, hf * 512:(hf + 1) * 512], ALU.add, [bp, bh[t]], [bh[t]])
            if t >= 1:
                norm_tile(t - 1)
        norm_tile(3)
        dump("h1", h[:], bh)
        dump("uT2", uT[:], buTt)

    def s_ffn():
        P.op("dve", lambda e: e.memset(actT[:, 0, 0:2], 0.0), [], bact_l)
        for c in range(NFF):
            wf, bwf = get_w(f"UG{c}")
            for _once in range(1):
                pg, bpg = PB()
                mms([(pg[:], wf[:, 0, kc, :], uT[:, kc, :], kc == 0, kc == 7) for kc in range(8)], buTt + bwf, [bpg])
                pu, bpu = PB()
                mms([(pu[:], wf[:, 1, kc, :], uT[:, kc, :], kc == 0, kc == 7) for kc in range(8)], buTt + bwf, [bpu])
                upb, bupb = TB()
                cp("act", upb[:], pu[:], [bpu], [bupb])
                gb_, bgb_ = gbuf[c % 2], bgb[c % 2]
                cp("dve", gb_[:, 0:2], ccarry[:, c, :], [bcc], [bgb_])
                cp("act", gb_[:, 2:TP + 2], pg[:], [bpg], [bgb_])
                cp("dve", ccarry[:, c, :], gb_[:, TP:TP + 2], [bgb_], [bcc])
                a1, ba1 = TF()
                act(a1[:], gb_[:, 0:TP], AF.Identity, [bgb_, bpar], [ba1], bias=cb[:, c:c + 1], scale=cw[:, c, 0:1])
                stt("dve", a1[:], gb_[:, 1:TP + 1], cw[:, c, 1:2], a1[:], ALU.mult, ALU.add, [bgb_, bpar, ba1], [ba1])
                stt("dve", a1[:], gb_[:, 2:TP + 2], cw[:, c, 2:3], a1[:], ALU.mult, ALU.add, [bgb_, bpar, ba1], [ba1])
                act(a1[:], a1[:], AF.Silu, [ba1], [ba1])
                tt("dve", actT[:, c, :], upb[:], a1[:], ALU.mult, [bupb, ba1], [bactc[c]])
        dump("actT", actT, bactc)

    def s_down_final(pi):
        def final_tile(t):
            k = t % 2
            act(ost[k][:], h[:, t, :], AF.Square, [bh[t]], [bost[k], bss[k]], accum_out=ss[:, k:k + 1])
            rstd_col(ss[:, k:k + 1], bss[k], DM)
            stt("dve", ost[k][:], h[:, t, :], ss[:, k:k + 1], gfb[:], ALU.mult, ALU.mult, [bh[t], bss[k], bpar], [bost[k]])
            r0 = pi * TP + t * 128
            P.dma("sp", y_d[r0:r0 + 128, :], ost[k][:], reads=[bost[k]], writes=[Buf()])

        for hf in range(2):
            banks = [PB() for _ in range(4)]
            for kp in range(2):
                wd, bwd = get_w(f"UD{hf}{kp}")
                for t in range(4):
                    pb, bp = banks[t]
                    mms([(v3(pb[:], 128), actT[:, kp * 11 + ki, t * 128:(t + 1) * 128], wd[:, :, ki, :],
                          (kp == 0 and ki == 0), (kp == 1 and ki == 10)) for ki in range(11)],
                        bactc[kp * 11:kp * 11 + 11] + [bWD[kp]], [bp], touch=bact_l + bWD_alias[kp])
                    if kp == 1:
                        tt("dve", h[:, t, hf * 512:(hf + 1) * 512], pb[:], h[:, t, hf * 512:(hf + 1) * 512], ALU.add,
                           [bp, bh[t]], [bh[t]])
                        if hf == 1:
                            final_tile(t)

    def emit_pass(pi):
        s_load_norm1(pi)
        s_attn_proj()
        for g in range(2):
            s_hgrn_proj(pi, g)
            for t in range(4):
                interleave([g_rec(g, t), g_attn(pi, t, g)])
        s_attn_carry()
        s_merge()
        s_outproj_norm2()
        s_ffn()
        s_down_final(pi)

    for pi in range(npass):
        emit_pass(pi)
    for s in range(len(P.dsem)):
        if P.dval[s]:
            P._wait("sp", ("d", s, P.dval[s]))
    print("instr counts", P.n)
    return nc


_NC_CACHE = {}


def make_in_maps(inputs, npass=NPASS_FULL, ncores=NCORES):
    x = np.ascontiguousarray(np.asarray(inputs["x"], dtype=np.float32))
    pos = np.ascontiguousarray(np.asarray(inputs["positions"], dtype=np.int32))
    B = x.shape[0]
    spc = B // ncores
    consts = make_consts()
    f = lambda k: np.ascontiguousarray(np.asarray(inputs[k], dtype=np.float32))
    common = {
        "w_in": f("w_in")[0], "w_a": f("w_a")[0], "w_b": f("w_b")[0], "w_out": f("w_out")[0],
        "w_ffn_in": f("w_ffn_in")[0], "w_down": f("w_down")[0],
        "g1c": np.ascontiguousarray(f("norm1_g")[0].reshape(8, 128).T),
        "g2c": np.ascontiguousarray(f("norm2_g")[0].reshape(8, 128).T),
        "gfb": np.ascontiguousarray(np.broadcast_to(f("final_g")[None, :], (128, DM))),
        "lbl": np.ascontiguousarray(f("lb_logits").reshape(2, 8, 128).transpose(2, 1, 0).reshape(128, 16)),
        "gn": np.ascontiguousarray(f("hgrn_norm_g")[0].reshape(128, 1)),
        "sinkl": np.ascontiguousarray(np.repeat(f("attn_sinks")[0].reshape(8, 2).T, 64, axis=0)),
        "cw": np.ascontiguousarray(f("conv_w")[0].reshape(3, NFF, 128).transpose(2, 1, 0).reshape(128, NFF * 3)),
        "cb": np.ascontiguousarray(f("conv_b")[0].reshape(NFF, 128).T),
    }
    common.update(consts)
    maps = []
    for c in range(ncores):
        xs = x[c * spc:(c + 1) * spc].reshape(spc * SEQ, DM)[: npass * TP]
        ps = pos[c * spc:(c + 1) * spc].reshape(spc * SEQ // TP, TP)[:npass]
        m = dict(common)
        m["x"] = np.ascontiguousarray(xs)
        m["pos"] = np.ascontiguousarray(ps)
        maps.append(m)
    return maps


def kernel(**inputs):
    if "nc" not in _NC_CACHE:
        _NC_CACHE["nc"] = build_program(NPASS_FULL)
    nc = _NC_CACHE["nc"]
    maps = make_in_maps(inputs)
    res = run_bass_kernel_spmd(nc, maps, core_ids=list(range(NCORES)))
    B = inputs["x"].shape[0]
    out = np.concatenate([np.asarray(r["y"], dtype=np.float32) for r in res.results], axis=0)
    return out.reshape(B, SEQ, DM)
```

```python
import numpy as np
import ml_dtypes
import concourse.bass as bass
import concourse.mybir as mybir
from concourse.bass_utils import run_bass_kernel_spmd

F32 = mybir.dt.float32
BF16 = mybir.dt.bfloat16
I32 = mybir.dt.int32
AF = mybir.ActivationFunctionType
ALU = mybir.AluOpType

NCORES = 8
SEQ = 2048
DM = 1024
TP = 512
NPASS_FULL = 16
D_IN = 7424
D_FF = 2816
NFF = 22
EPS = 1e-6
NEG = -30000.0


class Buf:
    __slots__ = ("w", "r", "name")

    def __init__(self, name=""):
        self.w = None
        self.r = {}
        self.name = name


class Prog:
    EPOCH = 30000

    def __init__(self, nc, n_dma_sems=24, self_sync=True):
        self.nc = nc
        self.E = {"pe": nc.tensor, "act": nc.scalar, "dve": nc.vector,
                  "pool": nc.gpsimd, "sp": nc.sync}
        self.sems = {e: [] for e in self.E}
        self.n = {e: 0 for e in self.E}
        self.seen = {e: {} for e in self.E}
        self.seen_d = {e: {} for e in self.E}
        self.n_hw = n_dma_sems
        self.n_sw = 8
        self.dsem = [nc.alloc_semaphore(f"dq{i}") for i in range(self.n_hw + self.n_sw)]
        self.dval = [0] * (self.n_hw + self.n_sw)
        self.dnext = {"hw": 0, "sw": 0}
        self.self_sync = self_sync

    def _sem(self, e, epoch):
        while len(self.sems[e]) <= epoch:
            self.sems[e].append(self.nc.alloc_semaphore(f"s_{e}_{len(self.sems[e])}"))
        return self.sems[e][epoch]

    def _wait(self, e, tok):
        if tok[0] == "c":
            _, oe, idx = tok
            if oe == e and not self.self_sync:
                return
            if self.seen[e].get(oe, -1) >= idx:
                return
            self.seen[e][oe] = idx
            self.E[e].wait_ge(self._sem(oe, idx // self.EPOCH), idx % self.EPOCH + 1)
        else:
            _, s, val = tok
            if self.seen_d[e].get(s, 0) >= val:
                return
            self.seen_d[e][s] = val
            self.E[e].wait_ge(self.dsem[s], val)

    def _deps(self, e, reads, writes, war_same=False):
        toks = []
        for b in reads:
            if b.w is not None:
                toks.append(b.w)
        for b in writes:
            if b.w is not None:
                toks.append(b.w)
            for t in b.r.values():
                toks.append(t)
        for t in toks:
            self._wait(e, t)

    def op(self, e, fn, reads=(), writes=(), touch=()):
        self._deps(e, reads, writes)
        ins = fn(self.E[e])
        idx = self.n[e]
        self.n[e] += 1
        ins.then_inc(self._sem(e, idx // self.EPOCH), 1)
        tok = ("c", e, idx)
        for b in reads:
            b.r[e] = tok
        for b in touch:
            b.r[e] = tok
        for b in writes:
            b.w = tok
            b.r = {}
        return tok

    def dma(self, q, out, in_, reads=(), writes=(), **kw):
        self._deps(q, reads, writes, war_same=True)
        if q == "pool":
            s = self.n_hw + self.dnext["sw"]
            self.dnext["sw"] = (self.dnext["sw"] + 1) % self.n_sw
        else:
            s = self.dnext["hw"]
            self.dnext["hw"] = (self.dnext["hw"] + 1) % self.n_hw
        if self.dval[s] > 0:
            self._wait(q, ("d", s, self.dval[s]))
        self.dval[s] += 16
        self.E[q].dma_start(out=out, in_=in_, **kw).then_inc(self.dsem[s], 16)
        tok = ("d", s, self.dval[s])
        for b in reads:
            b.r["dma%d" % s] = tok
        for b in writes:
            b.w = tok
            b.r = {}
        return tok

    def barrier(self):
        for e in self.E:
            for oe in self.E:
                if oe != e and self.n[oe] > 0:
                    self._wait(e, ("c", oe, self.n[oe] - 1))
            for s in range(len(self.dsem)):
                if self.dval[s]:
                    self._wait(e, ("d", s, self.dval[s]))


def _bf(a):
    return np.ascontiguousarray(a).astype(ml_dtypes.bfloat16)


def make_consts():
    c = {}
    c["c_ident"] = _bf(np.eye(128, dtype=np.float32))
    c["c_onesm"] = _bf(np.full((128, 128), 1.0 / 128, np.float32))
    s = np.arange(128)[:, None]
    q = np.arange(128)[None, :]
    hm = (s <= q).astype(np.float32)
    c["c_hmask4"] = _bf(np.tile(hm, (1, 4)))
    rm = np.ones((128, TP), np.float32)
    rm[:, ::128] = 0.0
    c["c_resetmask"] = _bf(rm)
    prev = np.where(s > q, 0.0, NEG).astype(np.float32)
    cur = np.where(s <= q, 0.0, NEG).astype(np.float32)
    full = np.full((128, 128), NEG, np.float32)
    c["c_maskN"] = _bf(np.concatenate([prev, cur, prev, cur], axis=1))
    c["c_maskF"] = _bf(np.concatenate([full, cur, full, cur], axis=1))
    half = 8
    inv = (np.float32(500000.0) ** (-2.0 * np.arange(half, dtype=np.float32) / np.float32(16))).astype(np.float32)
    invp = np.zeros((128, 1), np.float32)
    for p in range(128):
        d = p % 64
        if d < 16:
            invp[p, 0] = inv[d % 8]
    c["c_invp"] = invp
    Pm = np.zeros((128, 128), np.float32)
    for r0 in (0, 64):
        for j in range(8):
            Pm[r0 + j + 8, r0 + j] = -1.0
            Pm[r0 + j, r0 + j + 8] = 1.0
    c["c_pm"] = _bf(Pm)
    selA = np.zeros((4, 128, 128), np.float32)
    selB = np.zeros((4, 128, 128), np.float32)
    for kh in range(2):
        for hh in range(2):
            S = np.zeros((128, 128), np.float32)
            for i in range(64):
                S[kh * 64 + i, hh * 64 + i] = 1.0
            selA[kh * 2 + hh] = S
            selB[kh * 2 + hh] = Pm @ S
    c["c_selA"] = _bf(selA.transpose(1, 0, 2).reshape(128, 512))
    c["c_selB"] = _bf(selB.transpose(1, 0, 2).reshape(128, 512))
    ol = np.zeros((128, 256), np.float32)
    ol[:, 0:64] = 1.0
    ol[:, 128 + 64:256] = 1.0
    c["c_ones2"] = _bf(ol)
    return c


CONST_SHAPES = {
    "c_ident": ([128, 128], BF16), "c_onesm": ([128, 128], BF16), "c_hmask4": ([128, 512], BF16),
    "c_resetmask": ([128, TP], BF16), "c_maskN": ([128, 512], BF16), "c_maskF": ([128, 512], BF16),
    "c_invp": ([128, 1], F32), "c_pm": ([128, 128], BF16), "c_selA": ([128, 512], BF16),
    "c_selB": ([128, 512], BF16), "c_ones2": ([128, 256], BF16),
}

def make_units():
    units = []
    for g in range(2):
        units.append((f"UV{g}", [("w_in", 16 + 4 * g + i) for i in range(4)], list(range(8))))
    for h in range(8):
        units.append((f"UH{h}", [("w_in", h), ("w_in", 8 + h), ("w_in", 24 + h)], list(range(8))))
    units.append(("UQ0", [("w_in", 32 + i) for i in range(4)], list(range(8))))
    units.append(("UQ1", [("w_in", 36 + i) for i in range(4)], list(range(8))))
    units.append(("UKV", [("w_in", 40), ("w_in", 41)], list(range(8))))
    for j in range(8):
        units.append((f"UM{j}", [("w_in", 42 + j), ("w_in", 50 + j), ("w_a", j), ("w_b", j)], list(range(8))))
    for hf in range(2):
        units.append((f"UO{hf}", [("w_out", 4 * hf + i) for i in range(4)], list(range(8))))
    for c in range(NFF):
        units.append((f"UG{c}", [("w_ffn_in", c), ("w_ffn_in", 22 + c)], list(range(8))))
    for hf in range(2):
        for kp in range(2):
            units.append((f"UD{hf}{kp}", [("w_down", 4 * hf + i) for i in range(4)], list(range(11 * kp, 11 * kp + 11))))
    return units


WSHAPE = {"w_in": (1024, D_IN), "w_a": (1024, 1024), "w_b": (1024, 1024), "w_out": (1024, 1024),
          "w_ffn_in": (1024, 2 * D_FF), "w_down": (D_FF, 1024)}


def build_program(npass=NPASS_FULL, dbg=False):
    nc = bass.Bass("TRN2", target_bir_lowering=False)
    P = Prog(nc)
    dram_in = lambda name, shape, dt: nc.dram_tensor(name, shape, dt, kind="ExternalInput").ap()
    x_d = dram_in("x", [npass * TP, DM], F32)
    pos_d = dram_in("pos", [npass, TP], I32)
    W = {k: dram_in(k, list(v), F32) for k, v in WSHAPE.items()}
    gfb_d = dram_in("gfb", [128, DM], F32)
    lbl_d = dram_in("lbl", [128, 16], F32)
    gn_d = dram_in("gn", [128, 1], F32)
    sink_d = dram_in("sinkl", [128, 8], F32)
    cw_d = dram_in("cw", [128, NFF * 3], F32)
    cb_d = dram_in("cb", [128, NFF], F32)
    C_d = {k: dram_in(k, v[0], v[1]) for k, v in CONST_SHAPES.items()}
    y_d = nc.dram_tensor("y", [npass * TP, DM], F32, kind="ExternalOutput").ap()

    units = make_units()
    U_d = {}
    U_meta = {}
    segs = {}
    for name, chunks, kcs in units:
        U_d[name] = nc.dram_tensor("s_" + name, [128, len(chunks) * len(kcs) * 128], BF16, kind="Internal").ap()
        U_meta[name] = (chunks, len(kcs))
        for i, (w, jc) in enumerate(chunks):
            segs.setdefault((w, jc), []).append((name, i * len(kcs) * 128, kcs[0], len(kcs)))
    g1c_d = dram_in("g1c", [128, 8], F32)
    g2c_d = dram_in("g2c", [128, 8], F32)
    g1c = nc.alloc_sbuf_tensor("g1c_s", [128, 8], F32)
    g2c = nc.alloc_sbuf_tensor("g2c_s", [128, 8], F32)
    bgc = Buf()
    P.dma("sp", g1c[:], g1c_d, writes=[bgc])
    P.dma("sp", g2c[:], g2c_d, writes=[bgc])

    PW_MAX = 22528
    NLD = 4
    with nc.sbuf_tensor("stg_f", [128, NLD, 2048], F32) as stg_f, \
            nc.sbuf_tensor("stg_b", [128, 2, PW_MAX], BF16) as stg_b:
        bfl = [Buf() for _ in range(NLD)]
        bsb = [[Buf() for _ in range(22)] for _ in range(2)]
        ld = 0
        pci = 0
        for w, (K, N) in WSHAPE.items():
            KC = K // 128
            PW = 2048 if KC == 8 else 1024
            gcol = {"w_in": g1c, "w_ffn_in": g2c}.get(w)
            for c0 in range(0, N, PW):
                cw_ = min(PW, N - c0)
                nch = cw_ // 128
                si = pci % 2
                pci += 1
                blk = stg_b[:, si, 0:nch * KC * 128].rearrange("p (a b c) -> p a b c", b=KC, c=128)
                for kc in range(KC):
                    fi = ld % NLD
                    P.dma("sp", stg_f[:, fi, 0:cw_], W[w][kc * 128:(kc + 1) * 128, c0:c0 + cw_], writes=[bfl[fi]])
                    src = stg_f[:, fi, 0:cw_].rearrange("p (a c) -> p a c", c=128)
                    dst = blk[:, :, kc, :]
                    eng = "dve" if ld % 2 == 0 else "act"
                    if gcol is not None:
                        scale = gcol[:, kc:kc + 1]
                    elif w in ("w_a", "w_b"):
                        scale = 0.5
                    else:
                        scale = None
                    rd = [bfl[fi], bgc]
                    if eng == "dve":
                        if scale is None:
                            P.op("dve", lambda e, dst=dst, src=src: e.tensor_copy(dst, src), rd, [bsb[si][kc]])
                        else:
                            P.op("dve", lambda e, dst=dst, src=src, scale=scale: e.tensor_scalar(dst, src, scale, None, ALU.mult),
                                 rd, [bsb[si][kc]])
                    else:
                        if scale is None:
                            P.op("act", lambda e, dst=dst, src=src: e.activation(dst, src, AF.Copy), rd, [bsb[si][kc]])
                        elif isinstance(scale, float):
                            P.op("act", lambda e, dst=dst, src=src, scale=scale: e.activation(dst, src, AF.Copy, scale=scale),
                                 rd, [bsb[si][kc]])
                        else:
                            P.op("act", lambda e, dst=dst, src=src, scale=scale: e.activation(dst, src, AF.Identity, scale=scale),
                                 rd, [bsb[si][kc]])
                    ld += 1
                for jl in range(nch):
                    jc = c0 // 128 + jl
                    for (name, off, k0, nk) in segs.get((w, jc), []):
                        P.dma("pool", U_d[name][:, off:off + nk * 128],
                              blk[:, jl, k0:k0 + nk, :].rearrange("p a c -> p (a c)"),
                              reads=bsb[si][k0:k0 + nk], writes=[Buf()])
        P.barrier()

    sb = lambda name, shape, dt: nc.alloc_sbuf_tensor(name, shape, dt)
    cs = {k: sb(k + "_s", v[0], v[1]) for k, v in CONST_SHAPES.items()}
    bconst = Buf("const")
    gfb = sb("gfb_s", [128, DM], F32)
    lbl = sb("lbl_s", [128, 8, 2], F32)
    lbt = sb("lbt", [128, 8], F32)
    hs = sb("hs", [128, 8], F32)
    nhs = sb("nhs", [128, 8], F32)
    hb = sb("hb", [128, 8], F32)
    oml = sb("oml", [128, 8], F32)
    gn = sb("gn_s", [128, 1], F32)
    esink = sb("esink", [128, 8], F32)
    cw = sb("cw_s", [128, NFF, 3], F32)
    cb = sb("cb_s", [128, NFF], F32)

    h = sb("h", [128, 4, DM], F32)
    bh = [Buf(f"h{t}") for t in range(4)]
    uT = sb("uT", [128, 8, TP], BF16)
    buTt = [Buf(f"uT{t}") for t in range(4)]
    utok = [sb(f"utok{i}", [128, DM], BF16) for i in range(2)]
    butok = [Buf(), Buf()]
    ss = sb("ss", [128, 2], F32)
    bss = [Buf(), Buf()]
    ost = [sb(f"ost{i}", [128, DM], F32) for i in range(2)]
    bost = [Buf(), Buf()]
    posi_t = sb("posi", [128, TP], I32)
    kint_t = sb("kint", [128, TP], I32)
    bposi = Buf()
    bkint = Buf()
    ropeC = sb("ropeC", [128, TP], F32)
    ropeS = sb("ropeS", [128, TP], F32)
    brope = Buf()
    Rst = sb("Rst", [128, 8, 128], F32)
    Sbf = sb("Sbf", [128, 8, 128], BF16)
    bR = [Buf(f"R{i}") for i in range(8)]
    bS = [Buf("S0"), Buf("S1")]
    carry_sh = sb("carry_sh", [128, 8], F32)
    bcsh = Buf()
    Gall = sb("Gall", [128, 8, 8], F32)
    bG = Buf()
    sm = [sb(f"sm{i}", [128, 4, 8], F32) for i in range(2)]
    bsm = [Buf(), Buf()]
    X1 = sb("X1", [128, NFF * TP], BF16)
    X2 = sb("X2", [128, 3 * 8 * TP], BF16)
    x1v = lambda i: X1[:, i * 4 * TP:(i + 1) * 4 * TP].rearrange("p (a b) -> p a b", b=TP)
    q_in, k_in, q_out, shg, vg = x1v(0), x1v(1), x1v(2), x1v(3), x1v(4)
    bvg = Buf()
    bqk = [Buf() for _ in range(4)]
    ktok0 = [sb(f"ktok0_{i}", [128, 4, 128], BF16) for i in range(2)]
    bkt = [Buf(), Buf()]
    aTm = [X1[:, 20 * TP + i * 512:20 * TP + (i + 1) * 512].rearrange("p (a b) -> p a b", b=128) for i in range(2)]
    baT = [Buf(), Buf()]
    x2v = lambda i: X2[:, i * 8 * TP:(i + 1) * 8 * TP].rearrange("p (a b) -> p a b", b=TP)
    Qr, o_aT, o_bT = x2v(0), x2v(1), x2v(2)
    boa = Buf()
    bob = Buf()
    bQr = Buf()
    mT = Qr
    bmT = bQr
    Kpad = sb("Kpad", [128, 4, 640], BF16)
    bK = Buf()
    Vpad = sb("Vpad", [128, 4, 5, 128], BF16)
    bV = Buf()
    actT = X1[:, :].rearrange("p (a b) -> p a b", b=TP)
    bact_l = bqk + [bvg] + baT
    ccarry = sb("ccarry", [128, NFF, 2], F32)
    bcc = Buf()
    gbuf = [sb(f"gbuf{i}", [128, TP + 2], F32) for i in range(2)]
    bgb = [Buf(), Buf()]
    NT = 5
    tmpf = [sb(f"tmpf{i}", [128, TP], F32) for i in range(NT)]
    btf = [Buf() for _ in range(NT)]
    NH = 10
    htmp = [sb(f"htmp{i}", [128, TP], F32) for i in range(NH)]
    bht = [Buf() for _ in range(NH)]
    hfree = list(range(NH))

    def HA():
        i = hfree.pop(0)
        return htmp[i], bht[i], i

    def HF(i):
        assert i not in hfree
        hfree.append(i)
    NTB = 6
    tmpb = [sb(f"tmpb{i}", [128, TP], BF16) for i in range(NTB)]
    btb = [Buf() for _ in range(NTB)]
    tix = [0, 0]

    def TF():
        i = tix[0] % NT
        tix[0] += 1
        return tmpf[i], btf[i]

    def TB():
        i = tix[1] % NTB
        tix[1] += 1
        return tmpb[i], btb[i]

    WH = [sb(f"WH{i}", [128, 3, 8, 128], BF16) for i in range(2)]
    bWH = [Buf(), Buf()]
    WV = sb("WV", [128, 4, 8, 128], BF16)
    bWV = Buf()
    NWA = 3
    WA = [sb(f"WA{i}", [128, 4, 8, 128], BF16) for i in range(NWA)]
    bWA = [Buf() for _ in range(NWA)]
    WD = [X2[:, i * 5632:(i + 1) * 5632].rearrange("p (a b c) -> p a b c", b=11, c=128) for i in range(2)]
    bWD = [Buf("WD0"), Buf("WD1")]
    bWD_alias = [[bQr, boa], [boa, bob]]
    bactc = [Buf(f"act{c}") for c in range(NFF)]
    print("sbuf bytes remaining", nc.sbuf_bytes_remaining)

    NPB = 7
    pbk = [nc.alloc_psum_tensor(f"pb{i}", [128, 512], F32) for i in range(NPB)]
    bpb = [Buf() for _ in range(NPB)]
    ptall = nc.alloc_psum_tensor("ptall", [128, 1024], BF16)
    _bpt = Buf()
    ptk = [ptall[:, 0:512], ptall[:, 0:512]]
    bpt = [_bpt, _bpt]
    pix = [0, 0]

    pb_lim = [NPB]

    def PB():
        i = pix[0] % pb_lim[0]
        pix[0] += 1
        return pbk[i], bpb[i]

    def PT():
        i = pix[1] % 2
        pix[1] += 1
        return ptk[i], bpt[i]

    def act(out, in_, func, r, w, **kw):
        return P.op("act", lambda e: e.activation(out, in_, func, **kw), r, w)

    def tt(eng, out, a, b, op, r, w):
        return P.op(eng, lambda e: e.tensor_tensor(out, a, b, op), r, w)

    def ts(eng, out, a, s1, s2, op0, op1, r, w):
        if s2 is None:
            return P.op(eng, lambda e: e.tensor_scalar(out, a, s1, None, op0), r, w)
        return P.op(eng, lambda e: e.tensor_scalar(out, a, s1, s2, op0, op1), r, w)

    def stt(eng, out, a, s, b, op0, op1, r, w):
        return P.op(eng, lambda e: e.scalar_tensor_tensor(out, a, s, b, op0, op1), r, w)

    def cp(eng, out, in_, r, w):
        if eng == "act":
            return act(out, in_, AF.Copy, r, w)
        return P.op(eng, lambda e: e.tensor_copy(out, in_), r, w)

    def mms(items, r, w, touch=()):
        def fn(e):
            last = None
            for (o, l, rr, st, sp_) in items:
                last = e.matmul(o, l, rr, start=st, stop=sp_, skip_group_check=True)
            return last
        return P.op("pe", fn, r, w, touch)

    v3 = lambda ap, b: ap.rearrange("p (a b) -> p a b", b=b)
    dumped = {}

    def dump(name, ap, bufs):
        if not dbg or name in dumped:
            return
        d = nc.dram_tensor("d_" + name, list(ap.shape), ap.dtype, kind="ExternalOutput").ap()
        dumped[name] = d
        P.dma("sp", d, ap, reads=bufs, writes=[Buf()])

    def rstd_col(col, bcol, n):
        act(col, col, AF.Ln, [bcol], [bcol], bias=EPS, scale=1.0 / n)
        act(col, col, AF.Exp, [bcol], [bcol], scale=-0.5)

    for k in CONST_SHAPES:
        P.dma("sp", cs[k][:], C_d[k], writes=[bconst])
    bpar = Buf("par")
    P.dma("sp", gfb[:], gfb_d, writes=[bpar])
    P.dma("sp", lbl[:].rearrange("p a b -> p (a b)"), lbl_d, writes=[bpar])
    P.dma("sp", gn[:], gn_d, writes=[bpar])
    P.dma("sp", esink[:], sink_d, writes=[bpar])
    P.dma("sp", cw[:].rearrange("p a b -> p (a b)"), cw_d, writes=[bpar])
    P.dma("sp", cb[:], cb_d, writes=[bpar])
    tt("dve", lbt[:], lbl[:, :, 0], lbl[:, :, 1], ALU.subtract, [bpar], [bpar])
    act(lbt[:], lbt[:], AF.Tanh, [bpar], [bpar], scale=0.5)
    ts("dve", hs[:], lbt[:], -0.25, 0.25, ALU.mult, ALU.add, [bpar], [bpar])
    ts("dve", nhs[:], lbt[:], 0.25, -0.25, ALU.mult, ALU.add, [bpar], [bpar])
    ts("dve", hb[:], lbt[:], 0.25, 0.75, ALU.mult, ALU.add, [bpar], [bpar])
    act(esink[:], esink[:], AF.Exp, [bpar], [bpar])
    for i in range(2):
        P.op("pool", lambda e, i=i: e.memset(ktok0[i][:], 0.0), [], [bkt[i]])
    P.op("pool", lambda e: e.memset(Kpad[:], 0.0), [], [bK])
    P.op("pool", lambda e: e.memset(Vpad[:], 0.0), [], [bV])

    ident = cs["c_ident"]
    onesm = cs["c_onesm"]
    hmask4 = cs["c_hmask4"]
    resetmask = cs["c_resetmask"]
    maskN = cs["c_maskN"]
    maskF = cs["c_maskF"]
    invp = cs["c_invp"]
    pm = cs["c_pm"]
    selA = cs["c_selA"]
    selB = cs["c_selB"]
    ones2 = cs["c_ones2"]

    def pass_loads(pi):
        L = []
        L.append(("UQ0", "WA", None))
        L.append(("UQ1", "WA", None))
        L.append(("UKV", "WA", None))
        L.append(("UV0", "WV", 0))
        for hh in range(4):
            L.append((f"UH{hh}", "WH", hh % 2))
        L.append(("UV1", "WV", 0))
        for hh in range(4, 8):
            L.append((f"UH{hh}", "WH", hh % 2))
        for j in range(8):
            L.append((f"UM{j}", "WA", None))
        for hf in range(2):
            L.append((f"UO{hf}", "WA", None))
        for c in range(NFF):
            L.append((f"UG{c}", "WF", None))
        for hf in range(2):
            for kp in range(2):
                L.append((f"UD{hf}{kp}", "WD", kp))
        return L

    loads = []
    wa_ctr = 0
    for pi in range(npass):
        for (u, cls, idx) in pass_loads(pi):
            if cls == "WA":
                if wa_ctr % 2:
                    wa_ctr += 1
                idx = (wa_ctr // 2) % NWA
                wa_ctr += 2
            elif cls == "WF":
                idx = wa_ctr % (2 * NWA)
                wa_ctr += 1
            loads.append((u, cls, idx))

    def slots(ld):
        u, cls, idx = ld
        if cls == "WA":
            return {("WA", 2 * idx), ("WA", 2 * idx + 1)}
        if cls == "WF":
            return {("WA", idx)}
        return {(cls, idx)}
    lstate = {"issued": 0, "cur": 0}
    bWAh = [Buf(f"WAh{i}") for i in range(2 * NWA)]
    WFt = [WA[i // 2][:, (i % 2) * 2:(i % 2) * 2 + 2, :, :] for i in range(2 * NWA)]
    WBUF = {"WH": (WH, [[b] for b in bWH]), "WV": ([WV], [[bWV]]),
            "WA": (WA, [[bWAh[2 * i], bWAh[2 * i + 1]] for i in range(NWA)]),
            "WF": (WFt, [[b] for b in bWAh]), "WD": (WD, [[bWD[0]] + bWD_alias[0], [bWD[1]] + bWD_alias[1]])}

    def issue_load(k):
        u, cls, idx = loads[k]
        tiles, bufs = WBUF[cls]
        chunks, KC = U_meta[u]
        n = len(chunks) * KC * 128
        if cls == "WD":
            dst = X2[:, idx * 5632:idx * 5632 + n]
        elif cls == "WF":
            dst = WA[idx // 2][:].rearrange("p a b c -> p (a b c)")[:, (idx % 2) * 2048:(idx % 2) * 2048 + n]
        else:
            dst = tiles[idx][:].rearrange("p a b c -> p (a b c)")[:, 0:n]
        P.dma("sp", dst, U_d[u], writes=bufs[idx])

    def get_w(name, ahead=5, keep=0):
        k = lstate["cur"]
        assert loads[k][0] == name, (loads[k], name)
        lim = min(len(loads), k + 1 + ahead)
        while lstate["issued"] < lim:
            kk_ = lstate["issued"]
            if any(slots(loads[m]) & slots(loads[kk_]) for m in range(k - keep, kk_)):
                break
            issue_load(kk_)
            lstate["issued"] += 1
        assert lstate["issued"] > k
        lstate["cur"] += 1
        u, cls, idx = loads[k]
        tiles, bufs = WBUF[cls]
        return tiles[idx], bufs[idx]

    def norm_tile(t):
        k = t % 2
        act(ost[k][:], h[:, t, :], AF.Square, [bh[t]], [bost[k], bss[k]], accum_out=ss[:, k:k + 1])
        rstd_col(ss[:, k:k + 1], bss[k], DM)
        ts("dve", utok[k][:], h[:, t, :], ss[:, k:k + 1], None, ALU.mult, None,
           [bh[t], bss[k]], [butok[k]])
        for half in range(2):
            pt, bp = PT()

            def tr(e, k=k, half=half, pt=pt):
                last = None
                for i in range(4):
                    kc = half * 4 + i
                    last = e.transpose(pt[:, i * 128:(i + 1) * 128], utok[k][:, kc * 128:(kc + 1) * 128], ident[:])
                return last
            P.op("pe", tr, [butok[k], bconst], [bp])
            cp("act" if half == 0 else "dve", uT[:, half * 4:half * 4 + 4, t * 128:(t + 1) * 128],
               v3(pt[:, 0:512], 128), [bp], [buTt[t]])

    def norm_to_uT():
        for t in range(4):
            norm_tile(t)

    TWO_PI = float(2 * np.pi)
    PI_LO = 3.1415925

    def s_load_norm1(pi):
        first = (pi % 4 == 0)
        for t in range(4):
            r0 = pi * TP + t * 128
            P.dma("sp", h[:, t, :], x_d[r0:r0 + 128, :], writes=[bh[t]])
        posi = posi_t[:]
        kint = kint_t[:]
        P.dma("sp", posi, pos_d[pi:pi + 1, :].partition_broadcast(128), writes=[bposi])
        if first:
            P.op("pool", lambda e: e.memset(Rst[:], 0.0), [], bR)
            P.op("pool", lambda e: e.memset(Sbf[:], 0.0), [], bS)
            P.op("pool", lambda e: e.memset(carry_sh[:], 0.0), [], [bcsh])
            P.op("pool", lambda e: e.memset(ccarry[:], 0.0), [], [bcc])
        norm_to_uT()
        dump("uT", uT[:], buTt)
        posf, bposf = TF()
        cp("dve", posf[:], posi, [bposi], [bposf])
        for which, dst in ((0, ropeS), (1, ropeC)):
            ang, bang = TF()
            if which == 0:
                ts("dve", ang[:], posf[:], invp[:, 0:1], None, ALU.mult, None, [bposf, bconst], [bang])
            else:
                ts("dve", ang[:], posf[:], invp[:, 0:1], float(np.pi / 2), ALU.mult, ALU.add, [bposf, bconst], [bang])
            ts("dve", kint, ang[:], 1.0 / TWO_PI, None, ALU.mult, None, [bang], [bkint])
            kf, bkf = TF()
            cp("dve", kf[:], kint, [bkint], [bkf])
            stt("dve", ang[:], kf[:], -TWO_PI, ang[:], ALU.mult, ALU.add, [bkf, bang], [bang])
            ts("dve", ang[:], ang[:], PI_LO, -PI_LO, ALU.min, ALU.max, [bang], [bang])
            act(dst[:], ang[:], AF.Sin, [bang], [brope])

    def g_head(g, hl):
        hh = 4 * g + hl
        smi = sm[hl % 2]
        bsm_ = bsm[hl % 2]
        dd = smi[:, 0, 0:4]
        shv = smi[:, 1, 0:4]
        gg = smi[:, 2, 0:4]
        wh, bwh = get_w(f"UH{hh}")
        zs = []
        for ci in range(3):
            pb, bp = PB()
            mms([(pb[:], wh[:, ci, kc, :], uT[:, kc, :], kc == 0, kc == 7) for kc in range(8)],
                buTt + bwh, [bp])
            zs.append((pb, bp))
        (zq, bzq), (zf, bzf), (zg, bzg) = zs
        qf, bqf, iqf = HA()
        th, bth, ith = HA()
        lf, blf, ilf = HA()
        act(qf[:], zq[:], AF.Silu, [bzq], [bqf])
        act(shg[:, hl, :], zg[:], AF.Silu, [bzg], [bqk[hl]])
        act(th[:], zf[:], AF.Tanh, [bzf], [bth], scale=0.5)
        yield
        act(lf[:], th[:], AF.Ln, [bth, bpar], [blf], bias=hb[:, hh:hh + 1], scale=hs[:, hh:hh + 1])
        yield
        kk, bkk, ikk = HA()
        bb, bbb, ibb = HA()
        ts("dve", kk[:], th[:], nhs[:, hh:hh + 1], hs[:, hh:hh + 1], ALU.mult, ALU.add, [bth, bpar], [bkk])
        P.op("dve", lambda e: e.tensor_tensor_scan(bb[:], resetmask[:], lf[:], 0.0, ALU.mult, ALU.add),
             [blf, bconst], [bbb])
        b3 = v3(bb[:], 128)
        d1, bd1 = lf, blf
        tt("dve", v3(d1[:], 128), b3, b3[:, :, 64:65].broadcast_to([128, 4, 128]), ALU.subtract, [bbb], [bd1])
        tt("dve", dd, b3[:, :, 127], b3[:, :, 64], ALU.subtract, [bbb], [bsm_])
        cp("dve", shv[:, 0:1], carry_sh[:, hh:hh + 1], [bcsh, bsm_], [bsm_])
        cp("dve", shv[:, 1:4], dd[:, 0:3], [bsm_], [bsm_])
        cp("dve", carry_sh[:, hh:hh + 1], dd[:, 3:4], [bsm_], [bcsh])
        tt("dve", gg, b3[:, :, 64], shv, ALU.add, [bbb, bsm_], [bsm_])
        yield
        e2, be2 = th, bth
        act(e2[:], d1[:], AF.Exp, [bd1], [be2], scale=-1.0)
        act(Gall[:, hh, 0:4], gg, AF.Exp, [bsm_], [bG])
        tt("dve", k_in[:, hl, :], kk[:], e2[:], ALU.mult, [bkk, be2], [bqk[hl]])
        HF(ikk)
        HF(ith)
        e1, be1, ie1 = HA()
        act(e1[:], d1[:], AF.Exp, [bd1], [be1])
        tt("dve", q_in[:, hl, :], qf[:], e1[:], ALU.mult, [bqf, be1], [bqk[hl]])
        HF(ie1)
        d3, bd3 = d1, bd1
        tt("dve", v3(d3[:], 128), b3, shv.unsqueeze(2).broadcast_to([128, 4, 128]), ALU.add, [bbb, bsm_], [bd3])
        HF(ibb)
        yield
        e3, be3 = d3, bd3
        act(e3[:], d3[:], AF.Exp, [bd3], [be3])
        tt("dve", q_out[:, hl, :], qf[:], e3[:], ALU.mult, [bqf, be3], [bqk[hl]])
        HF(iqf)
        HF(ilf)
        yield

    def s_hgrn_proj(pi, g):
        wv, bwv = get_w(f"UV{g}")
        for t in range(4):
            pb, bp = PB()
            mms([(v3(pb[:], 128), uT[:, kc, t * 128:(t + 1) * 128], wv[:, :, kc, :], kc == 0, kc == 7)
                 for kc in range(8)], [buTt[t]] + bwv, [bp])
            cp("act", vg[:, t, :], pb[:], [bp], [bvg])
        gens = [g_head(g, hl) for hl in range(4)]
        sched = [(0, 0), (0, 1), (0, 2), (1, 0), (1, 1), (0, 3), (1, 2), (0, 4), (2, 0), (2, 1), (1, 3),
                 (2, 2), (1, 4), (3, 0), (3, 1), (2, 3), (3, 2), (2, 4), (3, 3), (3, 4)]
        for (hl, _seg) in sched:
            try:
                next(gens[hl])
            except StopIteration:
                pass
        dump("vg", vg, [bvg])
        dump("q_in", q_in, bqk)
        dump("k_in", k_in, bqk)
        dump("q_out", q_out, bqk)

    def g_rec(g, t):
        h0 = 4 * g
        tsl = slice(t * 128, (t + 1) * 128)
        par = t % 2
        pt, bp = PT()

        def trk(e, pt=pt, tsl=tsl):
            last = None
            for hl in range(4):
                last = e.transpose(pt[:, hl * 128:(hl + 1) * 128], k_in[:, hl, tsl], ident[:])
            return last
        P.op("pe", trk, bqk + [bconst], [bp])
        pA, bpA = pbk[0], bpb[0]
        mms([(pA[:, hl * 128:(hl + 1) * 128], k_in[:, hl, tsl], q_in[:, hl, tsl], True, True) for hl in range(4)],
            bqk, [bpA])
        yield
        cp("act", ktok0[par][:], v3(pt[:, 0:512], 128), [bp], [bkt[par]])
        tt("dve", aTm[par][:].rearrange("p a b -> p (a b)"), pA[:], hmask4[:], ALU.mult, [bpA, bconst], [baT[par]])
        pO, bpO = pbk[1], bpb[1]
        pK, bpK = pbk[2], bpb[2]
        items = []
        for hl in range(4):
            cs_ = slice(hl * 128, (hl + 1) * 128)
            items.append((pO[:, cs_], vg[:, t, cs_], aTm[par][:, hl, :], hl == 0, False))
            items.append((pO[:, cs_], Sbf[:, h0 + hl, :], q_out[:, hl, tsl], False, hl == 3))
        mms(items, [bvg, baT[par], bS[g]] + bqk, [bpO])
        mms([(pK[:, hl * 128:(hl + 1) * 128], ktok0[par][:, hl, :], vg[:, t, hl * 128:(hl + 1) * 128], True, True)
             for hl in range(4)], [bkt[par], bvg], [bpK])
        yield
        sq, bsq = TB()
        act(sq[:], pO[:], AF.Square, [bpO], [bsq])
        for hl in range(4):
            stt("dve", Rst[:, h0 + hl, :], Rst[:, h0 + hl, :], Gall[:, h0 + hl, t:t + 1],
                pK[:, hl * 128:(hl + 1) * 128], ALU.mult, ALU.add, [bR[h0 + hl], bG, bpK], [bR[h0 + hl]])
        cp("act", Sbf[:, h0:h0 + 4, :], Rst[:, h0:h0 + 4, :], bR[h0:h0 + 4], [bS[g]])
        pS, bpS = pbk[0], bpb[0]
        mms([(pS[:], onesm[:], sq[:], True, True)], [bsq, bconst], [bpS])
        yield
        rt, brt = TF()
        act(rt[:], pS[:], AF.Ln, [bpS], [brt], bias=EPS)
        act(rt[:], rt[:], AF.Exp, [brt], [brt], scale=-0.5)
        t1, bt1 = TF()
        stt("dve", t1[:], pO[:], gn[:, 0:1], rt[:], ALU.mult, ALU.mult, [bpO, brt, bpar], [bt1])
        tt("dve", o_aT[:, h0:h0 + 4, tsl], v3(t1[:], 128), shg[:, :, tsl], ALU.mult, [bt1] + bqk, [boa])

    def rope_split(z, bz):
        A, bA = TB()
        B, bB = TB()
        tt("dve", A[:], z[:], ropeC[:], ALU.mult, [bz, brope], [bA])
        tt("dve", B[:], z[:], ropeS[:], ALU.mult, [bz, brope], [bB])
        return A, bA, B, bB

    def s_attn_proj():
        pend = []

        def flush():
            kind, j, A, bA, B, bB = pend.pop(0)
            if kind == "q":
                pr, bpr = PB()
                mms([(pr[:], ident[:], A[:], True, False), (pr[:], pm[:], B[:], False, True)], [bA, bB, bconst], [bpr])
                act(Qr[:, j, :], pr[:], AF.Copy, [bpr], [bQr], scale=0.125)
            else:
                for var in range(4):
                    pr, bpr = PB()
                    mms([(pr[:], selA[:, var * 128:(var + 1) * 128], A[:], True, False),
                         (pr[:], selB[:, var * 128:(var + 1) * 128], B[:], False, True)], [bA, bB, bconst], [bpr])
                    cp("act", Kpad[:, var, 128:640], pr[:], [bpr], [bK])

        wkv = None
        for qi in range(2):
            wq, bwq = get_w(f"UQ{qi}")
            for jl in range(4):
                j = qi * 4 + jl
                pb, bp = PB()
                mms([(pb[:], wq[:, jl, kc, :], uT[:, kc, :], kc == 0, kc == 7) for kc in range(8)], buTt + bwq, [bp])
                A, bA, B, bB = rope_split(pb, bp)
                pend.append(("q", j, A, bA, B, bB))
                if len(pend) > 1:
                    flush()
        wkv, bwkv = get_w("UKV")
        pb, bp = PB()
        mms([(pb[:], wkv[:, 0, kc, :], uT[:, kc, :], kc == 0, kc == 7) for kc in range(8)], buTt + bwkv, [bp])
        A, bA, B, bB = rope_split(pb, bp)
        pend.append(("k", 0, A, bA, B, bB))
        flush()
        pb, bp = PB()
        items = []
        for t in range(4):
            for kc in range(8):
                items.append((pb[:, t * 128:(t + 1) * 128], uT[:, kc, t * 128:(t + 1) * 128], wkv[:, 1, kc, :],
                              (t == 0 and kc == 0), (t == 3 and kc == 7)))
        mms(items, buTt + bwkv, [bp])
        pb3 = v3(pb[:], 128)
        for kh in range(2):
            for hh in range(2):
                var = kh * 2 + hh
                cp("act" if hh == 0 else "dve", Vpad[:, var, 1:5, hh * 64:(hh + 1) * 64], pb3[:, :, kh * 64:(kh + 1) * 64], [bp], [bV])
        flush()
        dump("Qr", Qr, [bQr])
        dump("Kpad", Kpad[:], [bK])
        dump("Vpad", Vpad[:], [bV])

    def g_attn(pi, t, jb):
        first = (pi % 4 == 0)
        tsl = slice(t * 128, (t + 1) * 128)
        msk = maskF if (first and t == 0) else maskN
        pV, bpV = pbk[3], bpb[3]
        pD, bpD = pbk[4], bpb[4]
        kh = jb

        def pvden(jl, pT_, bpT):
            items = []
            items2 = []
            n_ = 0
            for hh in range(2):
                for kb in range(2):
                    r_ = (hh * 2 + kb)
                    st = (jl == 0 and n_ == 0)
                    items.append((pV[:, jl * 128:(jl + 1) * 128], Vpad[:, kh * 2 + hh, t + kb, :],
                                  pT_[:, r_ * 128:(r_ + 1) * 128], st, (jl == 3 and n_ == 3)))
                    items2.append((pD[:, jl * 128:(jl + 1) * 128], ones2[:, hh * 128:(hh + 1) * 128],
                                   pT_[:, r_ * 128:(r_ + 1) * 128], st, (jl == 3 and n_ == 3)))
                    n_ += 1
            mms(items, [bV, bpT], [bpV])
            mms(items2, [bconst, bpT], [bpD])

        prev = None
        for jl in range(4):
            j = jb * 4 + jl
            pSc, bpSc = pbk[5 + jl % 2], bpb[5 + jl % 2]
            items = [(pSc[:], ident[:], msk[:], True, False)]
            for hh in range(2):
                for kb in range(2):
                    r_ = (hh * 2 + kb)
                    items.append((pSc[:, r_ * 128:(r_ + 1) * 128], Kpad[:, kh * 2 + hh, (t + kb) * 128:(t + kb + 1) * 128],
                                  Qr[:, j, tsl], False, r_ == 3))
            mms(items, [bconst, bK, bQr], [bpSc])
            pT_, bpT = TB()
            act(pT_[:], pSc[:], AF.Exp, [bpSc], [bpT])
            if prev is not None:
                pvden(*prev)
            prev = (jl, pT_, bpT)
            yield
        pvden(*prev)
        yield
        dn, bdn = TF()
        for jl in range(4):
            j = jb * 4 + jl
            act(dn[:, jl * 128:(jl + 1) * 128], pD[:, jl * 128:(jl + 1) * 128], AF.Ln, [bpD, bpar], [bdn], bias=esink[:, j:j + 1])
        act(dn[:], dn[:], AF.Exp, [bdn], [bdn], scale=-1.0)
        tt("dve", o_bT[:, jb * 4:jb * 4 + 4, tsl], v3(pV[:], 128), v3(dn[:], 128), ALU.mult, [bpV, bdn], [bob])

    def interleave(gens):
        active = list(gens)
        while active:
            for g_ in list(active):
                try:
                    next(g_)
                except StopIteration:
                    active.remove(g_)

    def s_attn_carry():
        dump("o_bT", o_bT, [bob])
        cp("pool", Kpad[:, :, 0:128], Kpad[:, :, 512:640], [bK], [bK])
        cp("pool", Vpad[:, :, 0, :], Vpad[:, :, 4, :], [bV], [bV])

    def s_merge():
        dump("o_aT", o_aT, [boa])
        for j in range(8):
            wm, bwm = get_w(f"UM{j}")
            srcs = [(uT, buTt), (uT, buTt), (o_aT, [boa]), (o_bT, [bob])]
            ps_ = []
            for ci in range(4):
                pb, bp = PB()
                src, bsrc = srcs[ci]
                mms([(pb[:], wm[:, ci, kc, :], src[:, kc, :], kc == 0, kc == 7) for kc in range(8)], bsrc + bwm, [bp])
                ps_.append((pb, bp))
            ta, bta = TF()
            act(ta[:], ps_[0][0][:], AF.Tanh, [ps_[0][1]], [bta], scale=0.5)
            tb_, btb_ = TF()
            act(tb_[:], ps_[1][0][:], AF.Tanh, [ps_[1][1]], [btb_], scale=0.5)
            stt("dve", ta[:], ta[:], 1.0, ps_[2][0][:], ALU.add, ALU.mult, [bta, ps_[2][1]], [bta])
            stt("dve", tb_[:], tb_[:], 1.0, ps_[3][0][:], ALU.add, ALU.mult, [btb_, ps_[3][1]], [btb_])
            tt("dve", mT[:, j, :], ta[:], tb_[:], ALU.add, [bta, btb_], [bmT])
        dump("mT", mT[:], [bmT])

    def s_outproj_norm2():
        wos = [get_w("UO0"), get_w("UO1", keep=1)]
        for t in range(4):
            for hf in range(2):
                wo, bwo = wos[hf]
                pb, bp = PB()
                mms([(v3(pb[:], 128), mT[:, kc, t * 128:(t + 1) * 128], wo[:, :, kc, :], kc == 0, kc == 7) for kc in range(8)],
                    [bmT] + bwo, [bp])
                tt("dve", h[:, t, hf * 512:(hf + 1) * 512], pb[:], h[:, t, hf * 512:(hf + 1) * 512], ALU.add, [bp, bh[t]], [bh[t]])
            if t >= 1:
                norm_tile(t - 1)
        norm_tile(3)
        dump("h1", h[:], bh)
        dump("uT2", uT[:], buTt)

    def s_ffn():
        P.op("dve", lambda e: e.memset(actT[:, 0, 0:2], 0.0), [], bact_l)
        st = {}

        def stage_a(c):
            wf, bwf = get_w(f"UG{c}")
            pg, bpg = PB()
            mms([(pg[:], wf[:, 0, kc, :], uT[:, kc, :], kc == 0, kc == 7) for kc in range(8)], buTt + bwf, [bpg])
            pu, bpu = PB()
            mms([(pu[:], wf[:, 1, kc, :], uT[:, kc, :], kc == 0, kc == 7) for kc in range(8)], buTt + bwf, [bpu])
            upb, bupb = TB()
            gb_, bgb_ = gbuf[c % 2], bgb[c % 2]
            cp("dve", gb_[:, 0:2], ccarry[:, c, :], [bcc], [bgb_])
            cp("act", gb_[:, 2:TP + 2], pg[:], [bpg], [bgb_])
            cp("act", upb[:], pu[:], [bpu], [bupb])
            cp("dve", ccarry[:, c, :], gb_[:, TP:TP + 2], [bgb_], [bcc])
            a1, ba1 = TF()
            act(a1[:], gb_[:, 0:TP], AF.Identity, [bgb_, bpar], [ba1], bias=cb[:, c:c + 1], scale=cw[:, c, 0:1])
            st[c] = (upb, bupb, gb_, bgb_, a1, ba1)

        def stage_b(c):
            upb, bupb, gb_, bgb_, a1, ba1 = st.pop(c)
            stt("dve", a1[:], gb_[:, 1:TP + 1], cw[:, c, 1:2], a1[:], ALU.mult, ALU.add, [bgb_, bpar, ba1], [ba1])
            stt("dve", a1[:], gb_[:, 2:TP + 2], cw[:, c, 2:3], a1[:], ALU.mult, ALU.add, [bgb_, bpar, ba1], [ba1])
            act(a1[:], a1[:], AF.Silu, [ba1], [ba1])
            tt("dve", actT[:, c, :], upb[:], a1[:], ALU.mult, [bupb, ba1], [bactc[c]])

        for c in range(NFF):
            stage_a(c)
            if c >= 1:
                stage_b(c - 1)
        stage_b(NFF - 1)
        dump("actT", actT, bactc)

    def s_down_final(pi):
        def final_tile(t):
            k = t % 2
            act(ost[k][:], h[:, t, :], AF.Square, [bh[t]], [bost[k], bss[k]], accum_out=ss[:, k:k + 1])
            rstd_col(ss[:, k:k + 1], bss[k], DM)
            stt("dve", ost[k][:], h[:, t, :], ss[:, k:k + 1], gfb[:], ALU.mult, ALU.mult, [bh[t], bss[k], bpar], [bost[k]])
            r0 = pi * TP + t * 128
            P.dma("sp", y_d[r0:r0 + 128, :], ost[k][:], reads=[bost[k]], writes=[Buf()])

        for hf in range(2):
            banks = [PB() for _ in range(4)]
            for kp in range(2):
                wd, bwd = get_w(f"UD{hf}{kp}")
                for t in range(4):
                    pb, bp = banks[t]
                    mms([(v3(pb[:], 128), actT[:, kp * 11 + ki, t * 128:(t + 1) * 128], wd[:, :, ki, :],
                          (kp == 0 and ki == 0), (kp == 1 and ki == 10)) for ki in range(11)],
                        bactc[kp * 11:kp * 11 + 11] + [bWD[kp]], [bp], touch=bact_l + bWD_alias[kp])
                    if kp == 1:
                        tt("dve", h[:, t, hf * 512:(hf + 1) * 512], pb[:], h[:, t, hf * 512:(hf + 1) * 512], ALU.add,
                           [bp, bh[t]], [bh[t]])
                        if hf == 1:
                            final_tile(t)

    def emit_pass(pi):
        s_load_norm1(pi)
        s_attn_proj()
        for g in range(2):
            s_hgrn_proj(pi, g)
            for t in range(4):
                interleave([g_rec(g, t), g_attn(pi, t, g)])
        s_attn_carry()
        s_merge()
        s_outproj_norm2()
        s_ffn()
        s_down_final(pi)

    for pi in range(npass):
        emit_pass(pi)
    for s in range(len(P.dsem)):
        if P.dval[s]:
            P._wait("sp", ("d", s, P.dval[s]))
    print("instr counts", P.n)
    return nc


_NC_CACHE = {}


def make_in_maps(inputs, npass=NPASS_FULL, ncores=NCORES):
    x = np.ascontiguousarray(np.asarray(inputs["x"], dtype=np.float32))
    pos = np.ascontiguousarray(np.asarray(inputs["positions"], dtype=np.int32))
    B = x.shape[0]
    spc = B // ncores
    consts = make_consts()
    f = lambda k: np.ascontiguousarray(np.asarray(inputs[k], dtype=np.float32))
    common = {
        "w_in": f("w_in")[0], "w_a": f("w_a")[0], "w_b": f("w_b")[0], "w_out": f("w_out")[0],
        "w_ffn_in": f("w_ffn_in")[0], "w_down": f("w_down")[0],
        "g1c": np.ascontiguousarray(f("norm1_g")[0].reshape(8, 128).T),
        "g2c": np.ascontiguousarray(f("norm2_g")[0].reshape(8, 128).T),
        "gfb": np.ascontiguousarray(np.broadcast_to(f("final_g")[None, :], (128, DM))),
        "lbl": np.ascontiguousarray(f("lb_logits").reshape(2, 8, 128).transpose(2, 1, 0).reshape(128, 16)),
        "gn": np.ascontiguousarray(f("hgrn_norm_g")[0].reshape(128, 1)),
        "sinkl": np.ascontiguousarray(np.repeat(f("attn_sinks")[0].reshape(8, 2).T, 64, axis=0)),
        "cw": np.ascontiguousarray(f("conv_w")[0].reshape(3, NFF, 128).transpose(2, 1, 0).reshape(128, NFF * 3)),
        "cb": np.ascontiguousarray(f("conv_b")[0].reshape(NFF, 128).T),
    }
    common.update(consts)
    maps = []
    for c in range(ncores):
        xs = x[c * spc:(c + 1) * spc].reshape(spc * SEQ, DM)[: npass * TP]
        ps = pos[c * spc:(c + 1) * spc].reshape(spc * SEQ // TP, TP)[:npass]
        m = dict(common)
        m["x"] = np.ascontiguousarray(xs)
        m["pos"] = np.ascontiguousarray(ps)
        maps.append(m)
    return maps


def kernel(**inputs):
    if "nc" not in _NC_CACHE:
        _NC_CACHE["nc"] = build_program(NPASS_FULL)
    nc = _NC_CACHE["nc"]
    maps = make_in_maps(inputs)
    res = run_bass_kernel_spmd(nc, maps, core_ids=list(range(NCORES)))
    B = inputs["x"].shape[0]
    out = np.concatenate([np.asarray(r["y"], dtype=np.float32) for r in res.results], axis=0)
    return out.reshape(B, SEQ, DM)
```
